# Optimizing a Trainium2 kernel written in Bass

```python
import math
import jax, jax.numpy as jnp
from jax import lax
import numpy as np

D_MODEL = 2048
BATCH = 2
SEQ = 8192
DEPTH = 1

PLE_DIM = 256
D_MIX = D_MODEL
POOL_WIDTH = D_MIX // 2
POOL_WINDOWS = (2, 4, 8, 16)
POOL_GROUPS = len(POOL_WINDOWS)
POOL_GC = POOL_WIDTH // POOL_GROUPS
GLA_WIDTH = D_MIX - POOL_WIDTH
GLA_HEADS = 4
GLA_DV = GLA_WIDTH // GLA_HEADS
GLA_DK = GLA_DV // 2
GLA_KEY_WIDTH = GLA_HEADS * GLA_DK
GLA_GATE_RANK = 16
GLA_GATE_TEMP = 16.0
GLA_CHUNK = 64
PEER_HEADS = 8
PEER_NKEYS = 128
PEER_EXPERTS = PEER_NKEYS * PEER_NKEYS
PEER_DQ = 256
PEER_HALF = PEER_DQ // 2
PEER_TOPK = 16
PEER_TOKEN_BLOCK = 128
ALPHA = float((2 * DEPTH) ** 0.25)
BETA = float((8 * DEPTH) ** -0.25)
LN_EPS = 1e-5
RMS_EPS = 1e-6
D_IN = POOL_WIDTH + 2 * GLA_KEY_WIDTH + GLA_WIDTH + GLA_GATE_RANK + GLA_WIDTH

kernel_name = "hybrid_pool_gla_peer_deepnorm"


def layer_norm(x, w, b):
    xf = x.astype(jnp.float32)
    mu = jnp.mean(xf, axis=-1, keepdims=True)
    var = jnp.mean(jnp.square(xf - mu), axis=-1, keepdims=True)
    y = (xf - mu) * lax.rsqrt(var + LN_EPS)
    return (y * w.astype(jnp.float32) + b.astype(jnp.float32)).astype(x.dtype)


def pool_mixer(u, w_pool, scale):
    B, S, _ = u.shape
    uf = u.astype(jnp.float32).reshape(B, S, POOL_GROUPS, POOL_GC)
    cs = jnp.cumsum(uf, axis=1)
    pos = jnp.arange(1, S + 1, dtype=jnp.float32)
    outs = []
    for g, w in enumerate(POOL_WINDOWS):
        c = cs[:, :, g]
        lag = jnp.pad(c, ((0, 0), (w, 0), (0, 0)))[:, :S]
        cnt = jnp.minimum(pos, float(w))[None, :, None]
        outs.append((c - lag) / cnt - uf[:, :, g])
    d = jnp.stack(outs, axis=2).astype(u.dtype)
    y = jnp.einsum('bsgc,gcd->bsgd', d, w_pool)
    return y.reshape(B, S, POOL_WIDTH) * scale


def gla_chunked(q, k, v, g):
    B, S, H, DK = q.shape
    DV = v.shape[-1]
    nc = S // GLA_CHUNK

    def to_chunks(t):
        return t.reshape(B, nc, GLA_CHUNK, H, t.shape[-1]).transpose(1, 0, 3, 2, 4)

    qc, kc, vc, gc = (to_chunks(t.astype(jnp.float32)) for t in (q, k, v, g))
    causal = jnp.tril(jnp.ones((GLA_CHUNK, GLA_CHUNK), dtype=bool))[:, :, None]

    def step(state, inp):
        qb, kb, vb, gb = inp
        b = jnp.cumsum(gb, axis=2)
        diff = b[:, :, :, None, :] - b[:, :, None, :, :]
        decay = jnp.exp(jnp.where(causal, diff, -jnp.inf))
        attn = jnp.einsum('bhid,bhjd,bhijd->bhij', qb, kb, decay)
        o = (jnp.einsum('bhij,bhje->bhie', attn, vb)
             + jnp.einsum('bhid,bhde->bhie', qb * jnp.exp(b), state))
        b_last = b[:, :, -1:, :]
        new_state = (jnp.exp(b_last)[:, :, 0, :, None] * state
                     + jnp.einsum('bhjd,bhje->bhde', kb * jnp.exp(b_last - b), vb))
        return new_state, o

    state0 = jnp.zeros((B, H, DK, DV), jnp.float32)
    _, oc = lax.scan(step, state0, (qc, kc, vc, gc))
    return oc.transpose(1, 0, 3, 2, 4).reshape(B, S, H, DV)


def gla_mixer(q, k, v, glr, r, w_gate_up, b_gate, norm_w):
    B, S, _ = q.shape
    log_a = jax.nn.log_sigmoid((glr @ w_gate_up + b_gate).astype(jnp.float32)) / GLA_GATE_TEMP
    qh = q.reshape(B, S, GLA_HEADS, GLA_DK) * (GLA_DK ** -0.5)
    kh = k.reshape(B, S, GLA_HEADS, GLA_DK)
    vh = v.reshape(B, S, GLA_HEADS, GLA_DV)
    gh = log_a.reshape(B, S, GLA_HEADS, GLA_DK)
    o = gla_chunked(qh, kh, vh, gh)
    o = o * lax.rsqrt(jnp.mean(jnp.square(o), axis=-1, keepdims=True) + RMS_EPS)
    o = o * norm_w.astype(jnp.float32).reshape(GLA_HEADS, GLA_DV)
    o = o.reshape(B, S, GLA_WIDTH).astype(q.dtype)
    return o * jax.nn.silu(r)


def peer(x, w_query, sub_keys, u_tab, v_tab):
    B, S, D = x.shape
    T = B * S
    xt = x.reshape(T, D)
    q = (xt @ w_query).reshape(T, PEER_HEADS, 2, PEER_HALF).astype(jnp.float32)
    scores = jnp.einsum('thpc,hpnc->thpn', q, sub_keys.astype(jnp.float32))
    s, idx = lax.top_k(scores, PEER_TOPK)
    cand = s[:, :, 0, :, None] + s[:, :, 1, None, :]
    cand_idx = idx[:, :, 0, :, None] * PEER_NKEYS + idx[:, :, 1, None, :]
    cand = cand.reshape(T, PEER_HEADS, PEER_TOPK * PEER_TOPK)
    cand_idx = cand_idx.reshape(T, PEER_HEADS, PEER_TOPK * PEER_TOPK)
    top_s, sel = lax.top_k(cand, PEER_TOPK)
    expert = jnp.take_along_axis(cand_idx, sel, axis=-1)
    gate = jax.nn.softmax(top_s, axis=-1).astype(x.dtype)
    hk = PEER_HEADS * PEER_TOPK
    nb = T // PEER_TOKEN_BLOCK

    def block(args):
        xb, eb, gb = args
        u = jnp.take(u_tab, eb, axis=0)
        h = jax.nn.gelu(jnp.einsum('tkd,td->tk', u, xb), approximate=False)
        vv = jnp.take(v_tab, eb, axis=0)
        return jnp.einsum('tk,tkd->td', gb * h, vv)

    y = lax.map(block, (xt.reshape(nb, PEER_TOKEN_BLOCK, D),
                        expert.reshape(nb, PEER_TOKEN_BLOCK, hk),
                        gate.reshape(nb, PEER_TOKEN_BLOCK, hk)))
    return y.reshape(B, S, D)


def setup_inputs(seed: int = 0) -> dict:
    key = jax.random.key(seed)
    ks = jax.random.split(key, 20)
    n = jax.random.normal
    f32 = jnp.float32
    return {
        "x": n(ks[0], (BATCH, SEQ, D_MODEL), f32),
        "p": n(ks[1], (DEPTH, BATCH, SEQ, PLE_DIM), f32),
        "w_in": n(ks[2], (DEPTH, D_MODEL, D_IN), f32) * D_MODEL ** -0.5,
        "gla_w_gate_up": n(ks[3], (DEPTH, GLA_GATE_RANK, GLA_KEY_WIDTH), f32) * GLA_GATE_RANK ** -0.5,
        "gla_b_gate": 1.0 + 0.1 * n(ks[4], (DEPTH, GLA_KEY_WIDTH), f32),
        "gla_norm_w": 1.0 + 0.02 * n(ks[5], (DEPTH, GLA_WIDTH), f32),
        "pool_w": n(ks[6], (DEPTH, POOL_GROUPS, POOL_GC, POOL_GC), f32) * POOL_GC ** -0.5,
        "pool_scale": 1.0 + 0.02 * n(ks[7], (DEPTH, POOL_WIDTH), f32),
        "w_out": n(ks[8], (DEPTH, D_MIX, D_MODEL), f32) * (BETA * D_MIX ** -0.5),
        "ln1_w": 1.0 + 0.02 * n(ks[9], (DEPTH, D_MODEL), f32),
        "ln1_b": 0.02 * n(ks[10], (DEPTH, D_MODEL), f32),
        "peer_w_query": n(ks[11], (DEPTH, D_MODEL, PEER_HEADS * PEER_DQ), f32) * D_MODEL ** -0.5,
        "peer_sub_keys": n(ks[12], (DEPTH, PEER_HEADS, 2, PEER_NKEYS, PEER_HALF), f32) * PEER_HALF ** -0.5,
        "peer_u": n(ks[13], (DEPTH, PEER_EXPERTS, D_MODEL), f32) * D_MODEL ** -0.5,
        "peer_v": n(ks[14], (DEPTH, PEER_EXPERTS, D_MODEL), f32) * (BETA * PEER_HEADS ** -0.5),
        "ple_w_gate": n(ks[15], (DEPTH, D_MODEL, D_MODEL), f32) * D_MODEL ** -0.5,
        "ple_w_proj": n(ks[16], (DEPTH, PLE_DIM, D_MODEL), f32) * (BETA * PLE_DIM ** -0.5),
        "ln2_w": 1.0 + 0.02 * n(ks[17], (DEPTH, D_MODEL), f32),
        "ln2_b": 0.02 * n(ks[18], (DEPTH, D_MODEL), f32),
    }


def reference(x, p, w_in, gla_w_gate_up, gla_b_gate, gla_norm_w, pool_w, pool_scale,
              w_out, ln1_w, ln1_b, peer_w_query, peer_sub_keys, peer_u, peer_v,
              ple_w_gate, ple_w_proj, ln2_w, ln2_b):
    splits = np.cumsum([POOL_WIDTH, GLA_KEY_WIDTH, GLA_KEY_WIDTH, GLA_WIDTH, GLA_GATE_RANK]).tolist()
    for i in range(DEPTH):
        proj = x @ w_in[i]
        u_pool, q, k, v, glr, r = jnp.split(proj, splits, axis=-1)
        y_pool = pool_mixer(u_pool, pool_w[i], pool_scale[i])
        y_gla = gla_mixer(q, k, v, glr, r, gla_w_gate_up[i], gla_b_gate[i], gla_norm_w[i])
        mix = jnp.concatenate([y_pool, y_gla], axis=-1) @ w_out[i]
        x1 = layer_norm(ALPHA * x + mix, ln1_w[i], ln1_b[i])
        y_ffn = peer(x1, peer_w_query[i], peer_sub_keys[i], peer_u[i], peer_v[i])
        ple = jax.nn.sigmoid(x1 @ ple_w_gate[i]) * (p[i] @ ple_w_proj[i])
        x = layer_norm(ALPHA * x1 + y_ffn + ple, ln2_w[i], ln2_b[i])
    return x
```

```python
import numpy as np
from contextlib import ExitStack
import concourse.bass as bass
import concourse.mybir as mybir
from concourse.ap import AP
from concourse.bass_utils import run_bass_kernel_spmd

F32 = mybir.dt.float32
BF16 = mybir.dt.bfloat16
I32 = mybir.dt.int32
U32 = mybir.dt.uint32
AF = mybir.ActivationFunctionType
ALU = mybir.AluOpType
AX = mybir.AxisListType
DTSIZE = {F32: 4, BF16: 2, I32: 4, U32: 4}

D = 2048
KC = 16
DIN = 4112
C_U, C_Q, C_K, C_V, C_G, C_R = 0, 1024, 1536, 2048, 3072, 3088
DK = 128
ALPHA = float(2.0 ** 0.25)
LN_EPS = 1e-5
RMS_EPS = 1e-6
GATE_TEMP = 16.0
NG = 5
NEG = -1.0e30


class Dep:
    __slots__ = ("w", "r")

    def __init__(self):
        self.w = None
        self.r = {}


class Buf:
    def __init__(self, t, dep=None):
        self.t = t
        self.dep = dep or Dep()
        self.ps = t[:].ap[0][0]

    def __getitem__(self, k):
        return self.t[k]

    def v(self, off, *dims, p0=0, n=128):
        return AP(self.t, p0 * self.ps + off, [[self.ps, n]] + [list(d) for d in dims])


class Prog:
    ENG = ("pe", "act", "dve", "pool", "sp")

    def __init__(self, nc, es):
        self.nc = nc
        self.es = es
        self.sem = {e: es.enter_context(nc.semaphore("s_" + e)) for e in self.ENG}
        self.cnt = {e: 0 for e in self.ENG}
        self.known = {e: {} for e in self.ENG}
        self.ops = {e: [] for e in self.ENG}
        self.dsem = {}
        self.final = {}
        self.base = (nc.sbuf_base + 31) // 32 * 32
        self.top = nc.sbuf_top
        self.nalloc = 0

    def sb_at(self, off, shape, dt, name=None):
        self.nalloc += 1
        nbytes = int(np.prod(shape[1:])) * DTSIZE[dt]
        assert off % 32 == 0 and self.base + off + nbytes <= self.top, (name, off, nbytes, self.top - self.base)
        t = self.nc.alloc_sbuf_tensor_at(name or f"t{self.nalloc}", list(shape), dt, offset=self.base + off)
        return Buf(t)

    def _need(self, eng, reads, writes):
        need = {}
        for b in reads:
            t = b.dep.w
            if t is not None:
                need[t[0]] = max(need.get(t[0], 0), t[1])
        for b in writes:
            t = b.dep.w
            if t is not None:
                need[t[0]] = max(need.get(t[0], 0), t[1])
            for sm, v in b.dep.r.items():
                need[sm] = max(need.get(sm, 0), v)
        wl = []
        for sm, v in need.items():
            if eng == "pe" and sm == self.sem["pe"]:
                continue
            if self.known[eng].get(sm, 0) >= v:
                continue
            self.known[eng][sm] = v
            wl.append((sm, v))
        return wl

    def _reg(self, tok, reads, writes):
        for b in reads:
            b.dep.r[tok[0]] = max(b.dep.r.get(tok[0], 0), tok[1])
        for b in writes:
            b.dep.w = tok
            b.dep.r = {}

    def op(self, eng, fn, reads=(), writes=(), inc=True):
        wl = self._need(eng, reads, writes)
        if inc:
            self.cnt[eng] += 1
            tok = (self.sem[eng], self.cnt[eng])
            itok = tok
        else:
            tok = (self.sem[eng], self.cnt[eng] + 1)
            itok = None
        self.ops[eng].append((wl, fn, itok))
        self._reg(tok, reads, writes)
        return tok

    def dma(self, q, out_ap, in_ap, reads=(), writes=(), sem_buf=None, final=False, fn=None):
        key = (id(sem_buf.dep), q == "pool")
        if key not in self.dsem:
            self.dsem[key] = [self.es.enter_context(self.nc.semaphore(f"d{len(self.dsem)}")), 0]
        ent = self.dsem[key]
        wl = self._need(q, reads, writes)
        ent[1] += 16
        sem = ent[0]
        tok = (sem, ent[1])
        if fn is None:
            def fn(e, out_ap=out_ap, in_ap=in_ap):
                return e.dma_start(out=out_ap, in_=in_ap)

        def run(e, fn=fn, sem=sem):
            fn(e).then_inc(sem, 16)
            return None
        self.ops[q].append((wl, run, None))
        self._reg(tok, reads, writes)
        if final:
            self.final[sem] = max(self.final.get(sem, 0), tok[1])
        return tok

    def barrier(self):
        toks = [(self.sem[e], self.cnt[e]) for e in self.ENG if self.cnt[e] > 0]
        toks += [(ent[0], ent[1]) for ent in self.dsem.values() if ent[1] > 0]
        for e in self.ENG:
            wl = []
            for sm, v in toks:
                if sm == self.sem[e] and e == "pe":
                    continue
                if self.known[e].get(sm, 0) >= v:
                    continue
                self.known[e][sm] = v
                wl.append((sm, v))
            if wl:
                self.ops[e].append((wl, None, None))

    def emit(self):
        self.ops["sp"].append(([(sm, v) for sm, v in self.final.items()], None, None))
        esem = {self.sem[e]: e for e in self.ENG}
        waited = {e: set() for e in self.ENG}
        for e in self.ENG:
            for wl, fn, itok in self.ops[e]:
                for sm, v in wl:
                    if sm in esem:
                        waited[esem[sm]].add(v)
        remap = {}
        for e in self.ENG:
            n = 0
            m = {}
            for wl, fn, itok in self.ops[e]:
                if itok is not None and itok[1] in waited[e]:
                    n += 1
                    m[itok[1]] = n
            assert all(v in m for v in waited[e]), (e, sorted(waited[e] - set(m))[:5])
            remap[e] = m
        blk = self.es.enter_context(self.nc.Block())

        def runner(name):
            def f(e):
                for wl, fn, itok in self.ops[name]:
                    for sm, v in wl:
                        if sm in esem:
                            v = remap[esem[sm]][v]
                        e.wait_ge(sm, v)
                    if fn is not None:
                        ins = fn(e)
                        if itok is not None and itok[1] in remap[name]:
                            ins.then_inc(itok[0], 1)
            return f
        blk.tensor(runner("pe"))
        blk.scalar(runner("act"))
        blk.vector(runner("dve"))
        blk.gpsimd(runner("pool"))
        blk.sync(runner("sp"))


class DBuf:
    def __init__(self):
        self.dep = Dep()


def build_program(NT_OWN, NT_PRE):
    nc = bass.Bass("TRN2", target_bir_lowering=False)
    TO = NT_OWN * 128
    TPA = max(NT_PRE, 1) * 128

    def din(name, shape, dt=F32):
        return nc.dram_tensor(name, list(shape), dt, kind="ExternalInput").ap()

    xT = din("xT", [D, TO])
    xtok = din("xtok", [TO, D])
    xpT = din("xpT", [D, TPA])
    xhT = din("xhT", [D, 16])
    pT = din("pT", [256, TO])
    invc_d = din("invc", [128, 64])
    w_in = din("w_in", [D, DIN])
    w_out = din("w_out", [D, D])
    w_q = din("w_q", [D, D])
    w_g = din("w_g", [D, D])
    w_proj = din("w_proj", [256, D])
    pool_w = din("pool_w", [4, 256, 256])
    wgu_d = din("wgu", [17, 512])
    keysT_d = din("keysT", [128, 16 * 128])
    pscale_d = din("pscale", [128, 8])
    normw_d = din("normw", [128, 8])
    ln_d = [din(n, [1, D]) for n in ("ln1w", "ln1b", "ln2w", "ln2b")]
    peer_u = din("peer_u", [16384, D])
    peer_v = din("peer_v", [16384, D])
    c_triU = din("c_triU", [128, 128])
    c_triL = din("c_triL", [128, 128])
    c_ones = din("c_ones", [128, 128])
    c_ident = din("c_ident", [128, 128])
    c_iota = din("c_iota", [128, 16])
    out = nc.dram_tensor("out", [TO, D], F32, kind="ExternalOutput").ap()
    NGR = 20
    uvbf = nc.dram_tensor("uvbf_scr", [16384, 2 * D], BF16).ap()
    wsc = nc.dram_tensor("wsc_scr", [NGR, 128, 16 * 512], BF16).ap()

    es = ExitStack()
    with es:
        P = Prog(nc, es)
        cur = [0]

        def sb(shape, dt, name=None):
            n = int(np.prod(shape[1:])) * DTSIZE[dt]
            n = (n + 31) // 32 * 32
            b = P.sb_at(cur[0], shape, dt, name)
            cur[0] += n
            return b

        wgu = sb([32, 512], BF16, "wgu")
        glr_aug = sb([32, 128], BF16, "glr_aug")
        triU = sb([128, 128], F32, "triU")
        triL = sb([128, 128], F32, "triL")
        ones_f = sb([128, 128], F32, "ones_f")
        ident_f = sb([128, 128], F32, "ident_f")
        iota16 = sb([128, 16], F32, "iota16")
        poolw = sb([128, 4, 2, 256], BF16, "poolw")
        keysT = sb([128, 16, 128], BF16, "keysT")
        wproj = sb([128, 2, D], BF16, "wproj")
        wglr = sb([128, 16, 16], BF16, "wglr")
        lnt = [sb([128, D], F32, "ln%d" % i) for i in range(4)]
        pscale = sb([128, 8], F32, "pscale")
        normw = sb([128, 8], F32, "normw")
        invc = sb([128, 64], F32, "invc")
        S = sb([128, 4, 256], F32, "S")
        S_bf = sb([128, 4, 256], BF16, "S_bf")
        uhalo = sb([128, 8, 16], F32, "uhalo")
        st = sb([128, 4, 6], F32, "st")
        mv = sb([128, 2], F32, "mv")
        sd = sb([128, 2], F32, "sd")
        dec = sb([128, 4], F32, "dec")
        ident_bf = sb([128, 128], BF16, "ident_bf")
        xT_bf = sb([128, 16, 128], BF16, "xT_bf")
        w0_off = cur[0]
        wslot = [sb([128, 16, 512], BF16, "wslot0"), sb([128, 16, 512], BF16, "wslot1")]
        z_off = cur[0]
        z1 = sb([128, D], F32, "z1")
        z2 = sb([128, D], F32, "z2")
        wv1res = Buf(P.nc.alloc_sbuf_tensor_at("wv1res", [128, 16, 512], BF16, offset=P.base + z_off))
        tmpF = sb([128, D], F32, "tmpF")
        tmpG = sb([128, D], F32, "tmpG")
        phase0 = cur[0]
        uT = sb([128, 8, 144], F32, "uT")
        ptA = sb([128, 2, 144], F32, "ptA")
        ptB = sb([128, 2, 144], F32, "ptB")
        pfix = sb([128, 2, 16], F32, "pfix")
        dT = sb([128, 8, 128], BF16, "dT")
        qT = sb([128, 4, 128], F32, "qT")
        kT = sb([128, 4, 128], F32, "kT")
        srT = sb([128, 8, 128], F32, "srT")
        yT = sb([128, 16, 128], BF16, "yT")
        e1 = sb([128, 512], F32, "e1")
        sp_ = sb([128, 512], F32, "sp")
        ek = sb([128, 512], F32, "ek")
        eq = sb([128, 512], F32, "eq")
        ekk = sb([128, 512], F32, "ekk")
        ktl = sb([128, 512], BF16, "ktl")
        v_bf = sb([128, 1024], BF16, "v_bf")
        qtl = sb([128, 4, 128], BF16, "qtl")
        khat = sb([128, 4, 128], BF16, "khat")
        atm = sb([128, 4, 128], BF16, "atm")
        sq = sb([128, 8, 128], F32, "sq")
        on = sb([128, 8, 128], F32, "on")
        rstd = sb([128, 512], F32, "rstd")
        xh_bf = sb([128, 16, 16], BF16, "xh_bf")
        mixer_end = cur[0]
        cur[0] = phase0
        qpT = sb([128, 16, 128], BF16, "qpT")
        S_all = sb([128, 16, 128], F32, "S_all")
        m16 = sb([128, 16, 16], F32, "m16")
        i16 = sb([128, 16, 16], U32, "i16")
        work = sb([128, 256], F32, "work")
        iF = sb([128, 256], F32, "iF")
        ts = sb([128, 128], F32, "ts")
        sel = sb([128, 128], U32, "sel")
        aU = sb([128, 128], U32, "aU")
        bU = sb([128, 128], U32, "bU")
        aF = sb([128, 128], F32, "aF")
        bF = sb([128, 128], F32, "bF")
        i1g = sb([128, 128], F32, "i1g")
        i2g = sb([128, 128], F32, "i2g")
        eF = sb([128, 128], F32, "eF")
        idx = sb([128, 128], I32, "idx")
        ex = sb([128, 128], F32, "ex")
        gate = sb([128, 128], F32, "gate")
        hbuf = sb([128, 128], F32, "hbuf")
        cbuf = sb([128, 128], F32, "cbuf")
        ssm = sb([128, 8], F32, "ssm")
        sg = sb([128, 512], F32, "sg")
        ptmp = sb([128, 512], F32, "ptmp")
        pT_bf = sb([128, 2, 128], BF16, "pT_bf")
        tmpH = sb([128, D], F32, "tmpH")
        gbuf = [sb([128, 2 * D], BF16, "gbuf%d" % i) for i in range(NG)]
        diag = [sb([128, 128], BF16, "diag%d" % i) for i in range(4)]
        NHB = 8
        hb = [sb([128, 1], F32, "hb%d" % i) for i in range(NHB)]
        gh = [sb([128, 1], F32, "gh%d" % i) for i in range(NHB)]
        zcol = sb([128, 1], F32, "zcol")
        ffn_end = cur[0]
        assert max(mixer_end, ffn_end) + P.base <= P.top, (mixer_end, ffn_end, P.top - P.base)

        pb = [Buf(es.enter_context(nc.psum_tensor("pb%d" % i, [128, 512], F32))) for i in range(8)]

        def MM(o, lhsT, rhs, start, stop, reads, writes, inc=None):
            P.op("pe", lambda e: e.matmul(o, lhsT, rhs, start=start, stop=stop), reads, writes,
                 inc=stop if inc is None else inc)

        def ACT(o, i, func, reads, writes, **kw):
            P.op("act", lambda e: e.activation(out=o, in_=i, func=func, **kw), reads, writes)

        def TT(eng, o, a, b, op, reads, writes):
            P.op(eng, lambda e: e.tensor_tensor(out=o, in0=a, in1=b, op=op), reads, writes)

        def STT(eng, o, a, s, b, op0, op1, reads, writes, **kw):
            P.op(eng, lambda e: e.scalar_tensor_tensor(out=o, in0=a, scalar=s, in1=b, op0=op0, op1=op1, **kw), reads, writes)

        def TS(eng, o, a, s1, s2, op0, op1, reads, writes):
            if s2 is None:
                P.op(eng, lambda e: e.tensor_scalar(out=o, in0=a, scalar1=s1, scalar2=None, op0=op0), reads, writes)
            else:
                P.op(eng, lambda e: e.tensor_scalar(out=o, in0=a, scalar1=s1, scalar2=s2, op0=op0, op1=op1), reads, writes)

        def CP(eng, o, i, reads, writes):
            P.op(eng, lambda e: e.tensor_copy(out=o, in_=i), reads, writes)

        def wsrc(w, c0, n):
            return w.rearrange("(kc p) n -> p kc n", p=128)[:, :, c0:c0 + n]

        def bcast_row(d):
            return AP(d.tensor, 0, [[0, 128], [1, D]])

        for dst, src in ((triU, c_triU), (triL, c_triL), (ones_f, c_ones), (ident_f, c_ident),
                         (iota16, c_iota), (pscale, pscale_d), (normw, normw_d), (invc, invc_d)):
            P.dma("sp", dst[:], src, writes=[dst], sem_buf=dst)
        for i in range(4):
            P.dma("sp", lnt[i][:], bcast_row(ln_d[i]), writes=[lnt[i]], sem_buf=lnt[i])
        P.dma("pool", wgu[0:17, :], wgu_d, writes=[wgu], sem_buf=wgu)
        P.dma("pool", poolw[:], pool_w.rearrange("g (cc p) d -> p g cc d", p=128), writes=[poolw], sem_buf=poolw)
        P.dma("pool", keysT[:], keysT_d.rearrange("p (a n) -> p a n", a=16), writes=[keysT], sem_buf=keysT)
        P.dma("pool", wproj[:], w_proj.rearrange("(rc p) n -> p rc n", p=128), writes=[wproj], sem_buf=wproj)
        P.dma("pool", wglr[:], wsrc(w_in, C_G, 16), writes=[wglr], sem_buf=wglr)
        P.op("dve", lambda e: e.memset(glr_aug[:], 1.0), writes=[glr_aug])
        CP("dve", ident_bf[:], ident_f[:], [ident_f], [ident_bf])
        P.op("dve", lambda e: e.memset(S[:], 0.0), writes=[S])
        P.op("dve", lambda e: e.memset(S_bf[:], 0.0), writes=[S_bf])
        P.op("dve", lambda e: e.memset(uhalo[:], 0.0), writes=[uhalo])

        def load_xT(src, t0):
            P.dma("pool", xT_bf[:], src.rearrange("(kc p) t -> p kc t", p=128)[:, :, t0:t0 + 128],
                  writes=[xT_bf], sem_buf=xT_bf)

        def step_glr():
            for kc in range(KC):
                MM(pb[6][0:16, 0:128], wglr[:, kc, :], xT_bf[:, kc, :], kc == 0, kc == KC - 1,
                   [wglr, xT_bf], [pb[6]])
            ACT(glr_aug[0:16, :], pb[6][0:16, 0:128], AF.Copy, [pb[6]], [glr_aug])

        def step_ktok(Wk):
            for kc in range(KC):
                MM(pb[2][:, :], xT_bf[:, kc, :], Wk[:, kc, :], kc == 0, kc == KC - 1, [Wk, xT_bf], [pb[2]])

        def step_vtok(Wv, half):
            for kc in range(KC):
                MM(pb[3 + half][:, :], xT_bf[:, kc, :], Wv[:, kc, :], kc == 0, kc == KC - 1,
                   [Wv, xT_bf], [pb[3 + half]])

        def kv_rest(own):
            MM(pb[1][:, :], glr_aug[0:17, :], wgu[0:17, :], True, True, [glr_aug, wgu], [pb[1]])
            ACT(e1[:], pb[1][:, :], AF.Exp, [pb[1]], [e1], scale=-1.0)
            ACT(sp_[:], e1[:], AF.Ln, [e1], [sp_], bias=1.0)
            MM(pb[1][:, :], triL[:], sp_[:], True, True, [triL, sp_], [pb[1]])
            ACT(ek[:], pb[1][:, :], AF.Exp, [pb[1]], [ek], scale=-1.0 / GATE_TEMP)
            TT("dve", ktl[:], pb[2][:, :], ek[:], ALU.mult, [pb[2], ek], [ktl])
            ACT(v_bf[:, 0:512], pb[3][:, :], AF.Copy, [pb[3]], [v_bf])
            ACT(v_bf[:, 512:1024], pb[4][:, :], AF.Copy, [pb[4]], [v_bf])
            for h in range(4):
                MM(pb[5][:, h * 128:(h + 1) * 128], sp_[:, h * 128:(h + 1) * 128], triU[:], True, True,
                   [sp_, triU], [pb[5]])
            ACT(dec[:], pb[5].v(127, (128, 4)), AF.Exp, [pb[5]], [dec], scale=-1.0 / GATE_TEMP)
            if own:
                ACT(eq[:], pb[5][:, :], AF.Exp, [pb[5]], [eq], scale=-1.0 / GATE_TEMP)
                ACT(ekk[:], pb[5][:, :], AF.Exp, [pb[5]], [ekk], scale=1.0 / GATE_TEMP)
                STT("dve", qtl[:], qT[:], float(DK) ** -0.5, eq.v(0, (128, 4), (1, 128)), ALU.mult, ALU.mult,
                    [qT, eq], [qtl])
                TT("dve", khat[:], kT[:], ekk.v(0, (128, 4), (1, 128)), ALU.mult, [kT, ekk], [khat])
                for h in range(4):
                    MM(pb[6][:, h * 128:(h + 1) * 128], khat[:, h, :], qtl[:, h, :], True, True,
                       [khat, qtl], [pb[6]])
                TT("dve", atm[:], pb[6].v(0, (128, 4), (1, 128)), triU.v(0, (0, 4), (1, 128)), ALU.mult,
                   [pb[6], triU], [atm])
                for ch in range(8):
                    h, ec = ch // 2, ch % 2
                    bank = pb[0] if ch < 4 else pb[7]
                    o = bank[:, (ch % 4) * 128:(ch % 4 + 1) * 128]
                    MM(o, v_bf[:, h * 256 + ec * 128:h * 256 + (ec + 1) * 128], atm[:, h, :], True, False,
                       [v_bf, atm], [bank])
                    MM(o, S_bf[:, h, ec * 128:(ec + 1) * 128], qtl[:, h, :], False, True, [S_bf, qtl], [bank])
                ACT(sq[:, 0:4, :], pb[0].v(0, (128, 4), (1, 128)), AF.Square, [pb[0]], [sq])
                ACT(sq[:, 4:8, :], pb[7].v(0, (128, 4), (1, 128)), AF.Square, [pb[7]], [sq])
                for h in range(4):
                    MM(pb[6][:, h * 128:(h + 1) * 128], ones_f[:], sq[:, 2 * h, :], True, False, [ones_f, sq], [pb[6]])
                    MM(pb[6][:, h * 128:(h + 1) * 128], ones_f[:], sq[:, 2 * h + 1, :], False, True, [ones_f, sq], [pb[6]])
                ACT(rstd[:], pb[6][:, :], AF.Sqrt, [pb[6]], [rstd], bias=RMS_EPS, scale=1.0 / 256.0)
                P.op("dve", lambda e: e.reciprocal(out=rstd[:], in_=rstd[:]), [rstd], [rstd])
                TT("dve", on.v(0, (256, 2), (128, 2), (1, 128)), pb[0].v(0, (256, 2), (128, 2), (1, 128)),
                   rstd.v(0, (128, 2), (0, 2), (1, 128)), ALU.mult, [pb[0], rstd], [on])
                TT("dve", on.v(512, (256, 2), (128, 2), (1, 128)), pb[7].v(0, (256, 2), (128, 2), (1, 128)),
                   rstd.v(256, (128, 2), (0, 2), (1, 128)), ALU.mult, [pb[7], rstd], [on])
                for ch in range(8):
                    STT("dve", yT[:, 8 + ch, :], on[:, ch, :], normw[:, ch:ch + 1], srT[:, ch, :], ALU.mult, ALU.mult,
                        [on, normw, srT], [yT])
            for h in range(4):
                bank = pb[3] if h < 2 else pb[4]
                MM(bank[:, (h % 2) * 256:(h % 2 + 1) * 256], ktl[:, h * 128:(h + 1) * 128],
                   v_bf[:, h * 256:(h + 1) * 256], True, True, [ktl, v_bf], [bank])
            for h in range(4):
                bank = pb[3] if h < 2 else pb[4]
                STT("dve", S[:, h, :], S[:, h, :], dec[:, h:h + 1], bank[:, (h % 2) * 256:(h % 2 + 1) * 256],
                    ALU.mult, ALU.add, [S, dec, bank], [S])
            ACT(S_bf[:], S[:], AF.Copy, [S], [S_bf])

        groups = ([(w_in, c) for c in (C_U, C_U + 512, C_Q, C_K, C_V, C_V + 512, C_R, C_R + 512)]
                  + [(w_out, n * 512) for n in range(4)]
                  + [(w_q, n * 512) for n in range(4)]
                  + [(w_g, n * 512) for n in range(4)])
        assert len(groups) == NGR
        total_groups = NGR * NT_OWN
        ws_state = {"issued": 0, "next": 0}
        wsc_b = [DBuf() for _ in range(NGR)]
        uv_b = DBuf()
        cast_jobs = []
        for gi, (w, c0) in enumerate(groups):
            cast_jobs.append((wsc[gi].rearrange("p (kc n) -> p kc n", kc=16), wsrc(w, c0, 512), wsc_b[gi]))
        TCH = 16
        rows = 16384 // TCH
        for i in range(TCH):
            cast_jobs.append((uvbf[i * rows:(i + 1) * rows, 0:D], peer_u[i * rows:(i + 1) * rows, :], uv_b))
            cast_jobs.append((uvbf[i * rows:(i + 1) * rows, D:2 * D], peer_v[i * rows:(i + 1) * rows, :], uv_b))
        cast_state = {"i": 0}

        def cast_issue(n):
            for _ in range(n):
                i = cast_state["i"]
                if i >= len(cast_jobs):
                    return
                dst, src, db = cast_jobs[i]
                P.dma("pool", dst, src, writes=[db], sem_buf=db)
                cast_state["i"] += 1

        if NT_PRE > 0:
            Wk_r, Wv0_r, Wv1_r = wslot[0], wslot[1], wv1res
            P.dma("pool", Wk_r[:], wsrc(w_in, C_K, 512), writes=[Wk_r], sem_buf=Wk_r)
            P.dma("pool", Wv0_r[:], wsrc(w_in, C_V, 512), writes=[Wv0_r], sem_buf=Wv0_r)
            P.dma("pool", Wv1_r[:], wsrc(w_in, C_V + 512, 512), writes=[Wv1_r], sem_buf=Wv1_r)
            cast_issue(NGR)
            for it in range(NT_PRE):
                load_xT(xpT, it * 128)
                cast_issue(1)
                step_glr()
                step_ktok(Wk_r)
                step_vtok(Wv0_r, 0)
                step_vtok(Wv1_r, 1)
                kv_rest(False)
            cast_issue(len(cast_jobs))
            P.barrier()
        else:
            cast_issue(len(cast_jobs))

        def ws_issue():
            i = ws_state["issued"]
            if i >= total_groups:
                return
            gi = i % NGR
            slot = wslot[i % 2]
            P.dma("sp", slot[:], wsc[gi].rearrange("p (kc n) -> p kc n", kc=16), reads=[wsc_b[gi]], writes=[slot],
                  sem_buf=slot)
            ws_state["issued"] += 1

        def ws_get():
            i = ws_state["next"]
            while ws_state["issued"] <= i:
                ws_issue()
            ws_state["next"] += 1
            return wslot[i % 2]

        def ws_prefetch():
            while ws_state["issued"] <= ws_state["next"] and ws_state["issued"] < total_groups:
                ws_issue()

        def LN(Z, lw, lb, O):
            for c in range(4):
                P.op("dve", lambda e, c=c: e.bn_stats(out=st[:, c, :], in_=Z[:, c * 512:(c + 1) * 512]), [Z], [st])
            P.op("dve", lambda e: e.bn_aggr(out=mv[:], in_=st[:]), [st], [mv])
            ACT(sd[:, 0:1], mv[:, 1:2], AF.Sqrt, [mv], [sd], bias=LN_EPS, scale=1.0)
            P.op("dve", lambda e: e.reciprocal(out=sd[:, 1:2], in_=sd[:, 0:1]), [sd], [sd])
            TS("dve", tmpF[:], Z[:], mv[:, 0:1], sd[:, 1:2], ALU.subtract, ALU.mult, [Z, mv, sd], [tmpF])
            TT("dve", tmpF[:], tmpF[:], lw[:], ALU.mult, [tmpF, lw], [tmpF])
            TT("dve", O[:], tmpF[:], lb[:], ALU.add, [tmpF, lb], [O])

        for it in range(NT_OWN):
            t0 = it * 128
            load_xT(xT, t0)
            P.dma("sp", z1[:], xtok[t0:t0 + 128, :], writes=[z1], sem_buf=z1)
            if it == 0:
                P.dma("pool", xh_bf[:], xhT.rearrange("(kc p) t -> p kc t", p=128), writes=[xh_bf], sem_buf=xh_bf)
            else:
                CP("dve", uT[:, :, 0:16], uhalo[:], [uhalo], [uT])

            def wstat(W, nchunk, bank, evac):
                for ci in range(nchunk):
                    for kc in range(KC):
                        MM(bank[:, ci * 128:(ci + 1) * 128], W[:, kc, ci * 128:(ci + 1) * 128], xT_bf[:, kc, :],
                           kc == 0, kc == KC - 1, [W, xT_bf], [bank])
                evac(bank)

            def halo_u(W, cbase):
                for ci in range(4):
                    for kc in range(KC):
                        MM(pb[6][:, ci * 16:(ci + 1) * 16], W[:, kc, ci * 128:(ci + 1) * 128], xh_bf[:, kc, :],
                           kc == 0, kc == KC - 1, [W, xh_bf], [pb[6]])
                ACT(uT[:, cbase:cbase + 4, 0:16], pb[6].v(0, (16, 4), (1, 16)), AF.Copy, [pb[6]], [uT])

            for half in range(2):
                W = ws_get(); ws_prefetch()
                bank = pb[0] if half == 0 else pb[7]
                wstat(W, 4, bank, lambda bk, half=half: ACT(uT[:, half * 4:half * 4 + 4, 16:144],
                                                              bk.v(0, (128, 4), (1, 128)), AF.Copy, [bk], [uT]))
                if it == 0:
                    halo_u(W, half * 4)
            W = ws_get(); ws_prefetch()
            wstat(W, 4, pb[0], lambda bk: ACT(qT[:], bk.v(0, (128, 4), (1, 128)), AF.Copy, [bk], [qT]))
            W = ws_get(); ws_prefetch()
            wstat(W, 4, pb[7], lambda bk: ACT(kT[:], bk.v(0, (128, 4), (1, 128)), AF.Copy, [bk], [kT]))
            step_ktok(W)
            W = ws_get(); ws_prefetch()
            step_vtok(W, 0)
            W = ws_get(); ws_prefetch()
            step_vtok(W, 1)
            step_glr()
            for half in range(2):
                W = ws_get(); ws_prefetch()
                bank = pb[0] if half == 0 else pb[7]
                wstat(W, 4, bank, lambda bk, half=half: ACT(srT[:, half * 4:half * 4 + 4, :],
                                                              bk.v(0, (128, 4), (1, 128)), AF.Silu, [bk], [srT]))
            kv_rest(True)

            CP("dve", uhalo[:], uT[:, :, 128:144], [uT], [uhalo])
            for g in range(4):
                wdw = 2 ** (g + 1)
                src, so = uT, 2 * g * 144
                bufs = [ptA, ptB]
                lo = 0
                for lvl in range(g + 1):
                    sh = 2 ** lvl
                    lo2 = lo + sh
                    dst = bufs[lvl % 2]
                    n = 144 - lo2
                    TT("dve", dst.v(lo2, (144, 2), (1, n)), src.v(so + lo2, (144, 2), (1, n)),
                       src.v(so + lo2 - sh, (144, 2), (1, n)), ALU.add, [src], [dst])
                    src, so, lo = dst, 0, lo2
                STT("dve", dT[:, 2 * g:2 * g + 2, :], src.v(so + 16, (144, 2), (1, 128)), 1.0 / wdw,
                    uT.v(2 * g * 144 + 16, (144, 2), (1, 128)), ALU.mult, ALU.subtract, [src, uT], [dT])
                if it == 0:
                    TT("dve", pfix[:], src.v(so + 16, (144, 2), (1, 16)), invc.v(g * 16, (0, 2), (1, 16)), ALU.mult,
                       [src, invc], [pfix])
                    TT("dve", dT[:, 2 * g:2 * g + 2, 0:16], pfix[:], uT.v(2 * g * 144 + 16, (144, 2), (1, 16)),
                       ALU.subtract, [pfix, uT], [dT])
            for g in range(4):
                for dc in range(2):
                    c = 2 * g + dc
                    bank = pb[5] if c < 4 else pb[6]
                    o = bank[:, (c % 4) * 128:(c % 4 + 1) * 128]
                    for cc in range(2):
                        MM(o, poolw[:, g, cc, dc * 128:(dc + 1) * 128], dT[:, 2 * g + cc, :], cc == 0, cc == 1,
                           [poolw, dT], [bank])
            for c in range(8):
                bank = pb[5] if c < 4 else pb[6]
                TS("dve", yT[:, c, :], bank[:, (c % 4) * 128:(c % 4 + 1) * 128], pscale[:, c:c + 1], None,
                   ALU.mult, None, [bank, pscale], [yT])

            for n in range(4):
                W = ws_get(); ws_prefetch()
                bank = pb[1 + n]
                for c in range(KC):
                    MM(bank[:, :], yT[:, c, :], W[:, c, :], c == 0, c == KC - 1, [yT, W], [bank])
                STT("dve", z1[:, n * 512:(n + 1) * 512], z1[:, n * 512:(n + 1) * 512], ALPHA, bank[:, :],
                    ALU.mult, ALU.add, [z1, bank], [z1])
            P.barrier()

            P.dma("pool", pT_bf[:], pT.rearrange("(rc p) t -> p rc t", p=128)[:, :, t0:t0 + 128],
                  writes=[pT_bf], sem_buf=pT_bf)
            LN(z1, lnt[0], lnt[1], z1)
            for c in range(KC):
                bank = pb[1 + c // 4]
                P.op("pe", lambda e, c=c, bank=bank: e.transpose(out=bank[:, (c % 4) * 128:(c % 4 + 1) * 128],
                                                                  in_=z1[:, c * 128:(c + 1) * 128], identity=ident_f[:]),
                     [z1, ident_f], [bank])
            for q4 in range(4):
                ACT(xT_bf[:, q4 * 4:q4 * 4 + 4, :], pb[1 + q4].v(0, (128, 4), (1, 128)), AF.Copy, [pb[1 + q4]], [xT_bf])

            for n in range(4):
                W = ws_get(); ws_prefetch()
                bank = pb[5 + n % 2]
                for ci in range(4):
                    for kc in range(KC):
                        MM(bank[:, ci * 128:(ci + 1) * 128], W[:, kc, ci * 128:(ci + 1) * 128], xT_bf[:, kc, :],
                           kc == 0, kc == KC - 1, [W, xT_bf], [bank])
                ACT(qpT[:, n * 4:n * 4 + 4, :], bank.v(0, (128, 4), (1, 128)), AF.Copy, [bank], [qpT])
            for hp in range(16):
                bank = pb[1 + hp // 4]
                MM(bank[:, (hp % 4) * 128:(hp % 4 + 1) * 128], qpT[:, hp, :], keysT[:, hp, :], True, True,
                   [qpT, keysT], [bank])
            for q4 in range(4):
                ACT(S_all[:, q4 * 4:q4 * 4 + 4, :], pb[1 + q4].v(0, (128, 4), (1, 128)), AF.Copy, [pb[1 + q4]], [S_all])

            def ple_iter(n):
                W = ws_get(); ws_prefetch()
                bg = pb[1 + n % 2]
                bp = pb[3 + n % 2]
                for kc in range(KC):
                    MM(bg[:, :], xT_bf[:, kc, :], W[:, kc, :], kc == 0, kc == KC - 1, [xT_bf, W], [bg])
                for rc in range(2):
                    MM(bp[:, :], pT_bf[:, rc, :], wproj[:, rc, n * 512:(n + 1) * 512], rc == 0, rc == 1,
                       [pT_bf, wproj], [bp])
                ACT(sg[:], bg[:, :], AF.Sigmoid, [bg], [sg])
                TT("dve", ptmp[:], sg[:], bp[:, :], ALU.mult, [sg, bp], [ptmp])
                STT("dve", z2[:, n * 512:(n + 1) * 512], z1[:, n * 512:(n + 1) * 512], ALPHA, ptmp[:],
                    ALU.mult, ALU.add, [z1, ptmp], [z2])

            ple_iter(0)
            m16s = [Buf(m16.t) for _ in range(16)]
            i16s = [Buf(i16.t) for _ in range(16)]
            w16s = [Buf(tmpG.t) for _ in range(16)]

            def W16(hp):
                return tmpG[:, hp * 128:(hp + 1) * 128]
            for hp in range(16):
                P.op("dve", lambda e, hp=hp: e.max(out=m16[:, hp, 0:8], in_=S_all[:, hp, :]), [S_all], [m16s[hp]])
            for hp in range(16):
                P.op("dve", lambda e, hp=hp: e.max_index(out=i16[:, hp, 0:8], in_max=m16[:, hp, 0:8],
                                                          in_values=S_all[:, hp, :]), [S_all, m16s[hp]], [i16s[hp]])
            for hp in range(16):
                P.op("dve", lambda e, hp=hp: e.match_replace(out=W16(hp), in_to_replace=m16[:, hp, 0:8],
                                                              in_values=S_all[:, hp, :], imm_value=NEG),
                     [S_all, m16s[hp]], [w16s[hp]])
            for hp in range(16):
                P.op("dve", lambda e, hp=hp: e.max(out=m16[:, hp, 8:16], in_=W16(hp)), [w16s[hp]], [m16s[hp]])
            for hp in range(16):
                P.op("dve", lambda e, hp=hp: e.max_index(out=i16[:, hp, 8:16], in_max=m16[:, hp, 8:16],
                                                          in_values=W16(hp)), [w16s[hp], m16s[hp]], [i16s[hp]])
            ple_iter(1)
            TT("dve", tmpF.v(0, (256, 8), (16, 16), (1, 16)), m16.v(0, (32, 8), (1, 16), (0, 16)),
               m16.v(16, (32, 8), (0, 16), (1, 16)), ALU.add, m16s, [tmpF])
            ts_s = [Buf(ts.t) for _ in range(8)]
            sel_s = [Buf(sel.t) for _ in range(8)]
            w8s = [Buf(tmpH.t) for _ in range(8)]

            def C8(h):
                return tmpF[:, h * 256:(h + 1) * 256]

            def W8(h):
                return tmpH[:, h * 256:(h + 1) * 256]
            for h in range(8):
                P.op("dve", lambda e, h=h: e.max(out=ts[:, h * 16:h * 16 + 8], in_=C8(h)), [tmpF], [ts_s[h]])
            for h in range(8):
                P.op("dve", lambda e, h=h: e.max_index(out=sel[:, h * 16:h * 16 + 8], in_max=ts[:, h * 16:h * 16 + 8],
                                                        in_values=C8(h)), [tmpF, ts_s[h]], [sel_s[h]])
            for h in range(8):
                P.op("dve", lambda e, h=h: e.match_replace(out=W8(h), in_to_replace=ts[:, h * 16:h * 16 + 8],
                                                            in_values=C8(h), imm_value=NEG), [tmpF, ts_s[h]], [w8s[h]])
            for h in range(8):
                P.op("dve", lambda e, h=h: e.max(out=ts[:, h * 16 + 8:h * 16 + 16], in_=W8(h)), [w8s[h]], [ts_s[h]])
            for h in range(8):
                P.op("dve", lambda e, h=h: e.max_index(out=sel[:, h * 16 + 8:h * 16 + 16],
                                                        in_max=ts[:, h * 16 + 8:h * 16 + 16], in_values=W8(h)),
                     [w8s[h], ts_s[h]], [sel_s[h]])
            P.op("dve", lambda e: e.tensor_copy(out=ssm[:, 0:1], in_=ts[:, 0:1]),
                 ts_s + sel_s + w8s + w16s + i16s + m16s, [ssm, ts, sel, tmpH, tmpG, i16, m16])
            ple_iter(2)
            P.op("dve", lambda e: e.tensor_single_scalar(out=bU[:], in_=sel[:], scalar=15, op=ALU.bitwise_and), [sel], [bU])
            P.op("dve", lambda e: e.tensor_single_scalar(out=aU[:], in_=sel[:], scalar=4, op=ALU.logical_shift_right), [sel], [aU])
            CP("dve", aF[:], aU[:], [aU], [aF])
            CP("dve", bF[:], bU[:], [bU], [bF])
            CP("dve", iF[:], i16.v(0, (1, 256)), [i16], [iF])
            for which, xF, off, dst in ((0, aF, 0, i1g), (1, bF, 16, i2g)):
                TT("dve", tmpG.v(0, (256, 8), (16, 16), (1, 16)), xF.v(0, (16, 8), (1, 16), (0, 16)),
                   iota16.v(0, (0, 8), (0, 16), (1, 16)), ALU.is_equal, [xF, iota16], [tmpG])
                TT("dve", tmpH.v(0, (256, 8), (16, 16), (1, 16)), tmpG.v(0, (256, 8), (16, 16), (1, 16)),
                   iF.v(off, (32, 8), (0, 16), (1, 16)), ALU.mult, [tmpG, iF], [tmpH])
                P.op("dve", lambda e, dst=dst: e.reduce_sum(out=dst.v(0, (16, 8), (1, 16)),
                                                             in_=tmpH.v(0, (256, 8), (16, 16), (1, 16)), axis=AX.X),
                     [tmpH], [dst])
            STT("dve", eF[:], i1g[:], 128.0, i2g[:], ALU.mult, ALU.add, [i1g, i2g], [eF])
            CP("dve", idx[:], eF[:], [eF], [idx])
            ple_iter(3)
            TT("dve", ex.v(0, (16, 8), (1, 16)), ts.v(0, (16, 8), (1, 16)), ts.v(0, (16, 8), (0, 16)), ALU.subtract, [ts], [ex])
            ACT(ex[:], ex[:], AF.Exp, [ex], [ex])
            P.op("dve", lambda e: e.reduce_sum(out=ssm[:], in_=ex.v(0, (16, 8), (1, 16)), axis=AX.X), [ex], [ssm])
            P.op("dve", lambda e: e.reciprocal(out=ssm[:], in_=ssm[:]), [ssm], [ssm])
            TT("dve", gate.v(0, (16, 8), (1, 16)), ex.v(0, (16, 8), (1, 16)), ssm.v(0, (1, 8), (0, 16)), ALU.mult, [ex, ssm], [gate])

            P.op("dve", lambda e: e.memset(zcol[:], 0.0), [], [zcol])
            for i4 in range(NHB):
                P.op("dve", lambda e, i4=i4: e.memset(hb[i4][:], 0.0), [], [hb[i4]])

            def issue_gather(k):
                g = gbuf[k % NG]

                def fn(e, g=g, k=k):
                    return e.indirect_dma_start(out=g[:, :], out_offset=None, in_=uvbf[:, :],
                                                in_offset=bass.IndirectOffsetOnAxis(ap=idx[:, k:k + 1], axis=0))
                P.dma("pool", None, None, reads=[idx, uv_b], writes=[g], sem_buf=g, fn=fn)

            def dot(k):
                g = gbuf[k % NG]
                h_, g_ = hb[k % NHB], gh[k % NHB]
                jk = (tmpF, tmpG, tmpH)[k % 3]
                STT("dve", jk[:], g[:, 0:D], 1.0, z1[:], ALU.mult, ALU.mult, [g, z1], [jk, h_],
                    accum_out=h_[:])
                ACT(g_[:], h_[:], AF.Gelu, [h_], [g_])
                ACT(h_[:], zcol[:], AF.Copy, [zcol, g_], [h_])
                ACT(g_[:], g_[:], AF.Copy, [g_, gate], [g_], scale=gate[:, k:k + 1])
                ACT(diag[k % 4][:], ident_bf[:], AF.Copy, [ident_bf, g_], [diag[k % 4]], scale=g_[:, 0:1])

            def accum(k):
                g = gbuf[k % NG]
                dg = diag[k % 4]
                for n in range(4):
                    MM(pb[1 + n][:, :], dg[:], g[:, D + n * 512:D + (n + 1) * 512], k == 0, k == 127,
                       [dg, g], [pb[1 + n]], inc=(n == 3 or k == 127))

            for k in range(NG - 1):
                issue_gather(k)
            for k in range(128):
                dot(k)
                if k >= 1:
                    accum(k - 1)
                if k + NG - 1 < 128:
                    issue_gather(k + NG - 1)
            accum(127)
            for n in range(4):
                TT("dve", z2[:, n * 512:(n + 1) * 512], z2[:, n * 512:(n + 1) * 512], pb[1 + n][:, :], ALU.add,
                   [z2, pb[1 + n]], [z2])

            LN(z2, lnt[2], lnt[3], z2)
            P.dma("sp", out[t0:t0 + 128, :], z2[:], reads=[z2], sem_buf=z2, final=True)
            P.barrier()
        P.emit()
    return nc


def make_consts():
    s = np.arange(128)
    triU = (s[:, None] <= s[None, :]).astype(np.float32)
    triL = (s[:, None] > s[None, :]).astype(np.float32)
    return {"c_triU": triU, "c_triL": triL, "c_ones": np.ones((128, 128), np.float32),
            "c_ident": np.eye(128, dtype=np.float32),
            "c_iota": np.tile(np.arange(16, dtype=np.float32)[None, :], (128, 1))}


def shared_inputs(w_in, gla_w_gate_up, gla_b_gate, gla_norm_w, pool_w, pool_scale, w_out, ln1_w, ln1_b,
                  peer_w_query, peer_sub_keys, peer_u, peer_v, ple_w_gate, ple_w_proj, ln2_w, ln2_b):
    f = lambda a: np.ascontiguousarray(np.asarray(a, dtype=np.float32))
    m = dict(make_consts())
    m["w_in"] = f(w_in[0]); m["w_out"] = f(w_out[0]); m["w_q"] = f(peer_w_query[0]); m["w_g"] = f(ple_w_gate[0])
    m["w_proj"] = f(ple_w_proj[0]); m["pool_w"] = f(pool_w[0])
    m["wgu"] = f(np.concatenate([np.asarray(gla_w_gate_up[0]), np.asarray(gla_b_gate[0])[None, :]], axis=0))
    m["keysT"] = f(np.asarray(peer_sub_keys[0]).transpose(3, 0, 1, 2).reshape(128, 16 * 128))
    m["pscale"] = f(np.asarray(pool_scale[0]).reshape(8, 128).T)
    m["normw"] = f(np.asarray(gla_norm_w[0]).reshape(8, 128).T)
    m["ln1w"] = f(np.asarray(ln1_w[0])[None, :]); m["ln1b"] = f(np.asarray(ln1_b[0])[None, :])
    m["ln2w"] = f(np.asarray(ln2_w[0])[None, :]); m["ln2b"] = f(np.asarray(ln2_b[0])[None, :])
    m["peer_u"] = f(peer_u[0]); m["peer_v"] = f(peer_v[0])
    return m


def core_inputs(xb, pb_, s0, TO, TP):
    f = lambda a: np.ascontiguousarray(a, dtype=np.float32)
    m = {}
    m["xtok"] = f(xb[s0:s0 + TO])
    m["xT"] = f(xb[s0:s0 + TO].T)
    TPA = max(TP, 128)
    xp = np.zeros((D, TPA), np.float32)
    npre = min(s0, TP)
    if npre > 0:
        xp[:, TPA - npre:] = xb[s0 - npre:s0].T
    m["xpT"] = xp
    xh = np.zeros((D, 16), np.float32)
    if s0 >= 16:
        xh[:, :] = xb[s0 - 16:s0].T
    m["xhT"] = xh
    m["pT"] = f(pb_[s0:s0 + TO].T)
    ic = np.zeros((128, 64), np.float32)
    for g in range(4):
        w = 2 ** (g + 1)
        for t in range(16):
            ic[:, g * 16 + t] = 1.0 / (min(t + 1, w) if s0 == 0 else w)
    m["invc"] = ic
    return m


_NC_CACHE = {}


def kernel(x, p, w_in, gla_w_gate_up, gla_b_gate, gla_norm_w, pool_w, pool_scale, w_out, ln1_w, ln1_b,
           peer_w_query, peer_sub_keys, peer_u, peer_v, ple_w_gate, ple_w_proj, ln2_w, ln2_b):
    x = np.asarray(x, dtype=np.float32)
    p = np.asarray(p, dtype=np.float32)
    B, S, _ = x.shape
    NCORE = 8
    per = (B * S) // NCORE
    cps = S // per
    NT_OWN = per // 128
    NT_PRE = (cps - 1) * NT_OWN
    sh = shared_inputs(w_in, gla_w_gate_up, gla_b_gate, gla_norm_w, pool_w, pool_scale, w_out, ln1_w, ln1_b,
                       peer_w_query, peer_sub_keys, peer_u, peer_v, ple_w_gate, ple_w_proj, ln2_w, ln2_b)
    in_maps = []
    for c in range(NCORE):
        b, j = c // cps, c % cps
        m = dict(sh)
        m.update(core_inputs(x[b], p[0, b], j * per, per, NT_PRE * 128))
        in_maps.append(m)
    key = (NT_OWN, NT_PRE)
    if key not in _NC_CACHE:
        _NC_CACHE[key] = build_program(NT_OWN, NT_PRE)
    nc = _NC_CACHE[key]
    res = run_bass_kernel_spmd(nc, in_maps, core_ids=list(range(NCORE)))
    outs = [np.asarray(r["out"], dtype=np.float32) for r in res.results]
    return np.concatenate(outs, axis=0).reshape(B, S, D)
```

```python
import numpy as np
from contextlib import ExitStack
import concourse.bass as bass
import concourse.mybir as mybir
from concourse.ap import AP
from concourse.bass_utils import run_bass_kernel_spmd

F32 = mybir.dt.float32
BF16 = mybir.dt.bfloat16
I32 = mybir.dt.int32
U32 = mybir.dt.uint32
AF = mybir.ActivationFunctionType
ALU = mybir.AluOpType
AX = mybir.AxisListType
DTSIZE = {F32: 4, BF16: 2, I32: 4, U32: 4}

D = 2048
KC = 16
DIN = 4112
C_U, C_Q, C_K, C_V, C_G, C_R = 0, 1024, 1536, 2048, 3072, 3088
DK = 128
ALPHA = float(2.0 ** 0.25)
LN_EPS = 1e-5
RMS_EPS = 1e-6
GATE_TEMP = 16.0
NG = 5
NEG = -1.0e30


class Dep:
    __slots__ = ("w", "r")

    def __init__(self):
        self.w = None
        self.r = {}


class Buf:
    def __init__(self, t, dep=None):
        self.t = t
        self.dep = dep or Dep()
        self.ps = t[:].ap[0][0]

    def __getitem__(self, k):
        return self.t[k]

    def v(self, off, *dims, p0=0, n=128):
        return AP(self.t, p0 * self.ps + off, [[self.ps, n]] + [list(d) for d in dims])


class Prog:
    ENG = ("pe", "act", "dve", "pool", "sp")

    def __init__(self, nc, es):
        self.nc = nc
        self.es = es
        self.sem = {e: es.enter_context(nc.semaphore("s_" + e)) for e in self.ENG}
        self.cnt = {e: 0 for e in self.ENG}
        self.known = {e: {} for e in self.ENG}
        self.ops = {e: [] for e in self.ENG}
        self.dsem = {}
        self.final = {}
        self.base = (nc.sbuf_base + 31) // 32 * 32
        self.top = nc.sbuf_top
        self.nalloc = 0

    def sb_at(self, off, shape, dt, name=None):
        self.nalloc += 1
        nbytes = int(np.prod(shape[1:])) * DTSIZE[dt]
        assert off % 32 == 0 and self.base + off + nbytes <= self.top, (name, off, nbytes, self.top - self.base)
        t = self.nc.alloc_sbuf_tensor_at(name or f"t{self.nalloc}", list(shape), dt, offset=self.base + off)
        return Buf(t)

    def _need(self, eng, reads, writes):
        need = {}
        for b in reads:
            t = b.dep.w
            if t is not None:
                need[t[0]] = max(need.get(t[0], 0), t[1])
        for b in writes:
            t = b.dep.w
            if t is not None:
                need[t[0]] = max(need.get(t[0], 0), t[1])
            for sm, v in b.dep.r.items():
                need[sm] = max(need.get(sm, 0), v)
        wl = []
        for sm, v in need.items():
            if eng == "pe" and sm == self.sem["pe"]:
                continue
            if self.known[eng].get(sm, 0) >= v:
                continue
            self.known[eng][sm] = v
            wl.append((sm, v))
        return wl

    def _reg(self, tok, reads, writes):
        for b in reads:
            b.dep.r[tok[0]] = max(b.dep.r.get(tok[0], 0), tok[1])
        for b in writes:
            b.dep.w = tok
            b.dep.r = {}

    def op(self, eng, fn, reads=(), writes=(), inc=True):
        wl = self._need(eng, reads, writes)
        if inc:
            self.cnt[eng] += 1
            tok = (self.sem[eng], self.cnt[eng])
            itok = tok
        else:
            tok = (self.sem[eng], self.cnt[eng] + 1)
            itok = None
        self.ops[eng].append((wl, fn, itok))
        self._reg(tok, reads, writes)
        return tok

    def dma(self, q, out_ap, in_ap, reads=(), writes=(), sem_buf=None, final=False, fn=None):
        key = (id(sem_buf.dep), q == "pool")
        if key not in self.dsem:
            self.dsem[key] = [self.es.enter_context(self.nc.semaphore(f"d{len(self.dsem)}")), 0]
        ent = self.dsem[key]
        wl = self._need(q, reads, writes)
        ent[1] += 16
        sem = ent[0]
        tok = (sem, ent[1])
        if fn is None:
            def fn(e, out_ap=out_ap, in_ap=in_ap):
                return e.dma_start(out=out_ap, in_=in_ap)

        def run(e, fn=fn, sem=sem):
            fn(e).then_inc(sem, 16)
            return None
        self.ops[q].append((wl, run, None))
        self._reg(tok, reads, writes)
        if final:
            self.final[sem] = max(self.final.get(sem, 0), tok[1])
        return tok

    def barrier(self):
        toks = [(self.sem[e], self.cnt[e]) for e in self.ENG if self.cnt[e] > 0]
        toks += [(ent[0], ent[1]) for ent in self.dsem.values() if ent[1] > 0]
        for e in self.ENG:
            wl = []
            for sm, v in toks:
                if sm == self.sem[e] and e == "pe":
                    continue
                if self.known[e].get(sm, 0) >= v:
                    continue
                self.known[e][sm] = v
                wl.append((sm, v))
            if wl:
                self.ops[e].append((wl, None, None))

    def emit(self):
        self.ops["sp"].append(([(sm, v) for sm, v in self.final.items()], None, None))
        esem = {self.sem[e]: e for e in self.ENG}
        waited = {e: set() for e in self.ENG}
        for e in self.ENG:
            for wl, fn, itok in self.ops[e]:
                for sm, v in wl:
                    if sm in esem:
                        waited[esem[sm]].add(v)
        remap = {}
        for e in self.ENG:
            n = 0
            m = {}
            for wl, fn, itok in self.ops[e]:
                if itok is not None and itok[1] in waited[e]:
                    n += 1
                    m[itok[1]] = n
            assert all(v in m for v in waited[e]), (e, sorted(waited[e] - set(m))[:5])
            remap[e] = m
        blk = self.es.enter_context(self.nc.Block())

        def runner(name):
            def f(e):
                for wl, fn, itok in self.ops[name]:
                    for sm, v in wl:
                        if sm in esem:
                            v = remap[esem[sm]][v]
                        e.wait_ge(sm, v)
                    if fn is not None:
                        ins = fn(e)
                        if itok is not None and itok[1] in remap[name]:
                            ins.then_inc(itok[0], 1)
            return f
        blk.tensor(runner("pe"))
        blk.scalar(runner("act"))
        blk.vector(runner("dve"))
        blk.gpsimd(runner("pool"))
        blk.sync(runner("sp"))


class DBuf:
    def __init__(self):
        self.dep = Dep()


def build_program(NT_OWN, NT_PRE):
    nc = bass.Bass("TRN2", target_bir_lowering=False)
    TO = NT_OWN * 128
    TPA = max(NT_PRE, 1) * 128

    def din(name, shape, dt=F32):
        return nc.dram_tensor(name, list(shape), dt, kind="ExternalInput").ap()

    xT = din("xT", [D, TO])
    xtok = din("xtok", [TO, D])
    xpT = din("xpT", [D, TPA])
    xhT = din("xhT", [D, 16])
    pT = din("pT", [256, TO])
    invc_d = din("invc", [128, 64])
    w_in = din("w_in", [D, DIN])
    w_out = din("w_out", [D, D])
    w_q = din("w_q", [D, D])
    w_g = din("w_g", [D, D])
    w_proj = din("w_proj", [256, D])
    pool_w = din("pool_w", [4, 256, 256])
    wgu_d = din("wgu", [17, 512])
    keysT_d = din("keysT", [128, 16 * 128])
    pscale_d = din("pscale", [128, 8])
    normw_d = din("normw", [128, 8])
    ln_d = [din(n, [1, D]) for n in ("ln1w", "ln1b", "ln2w", "ln2b")]
    peer_u = din("peer_u", [16384, D])
    peer_v = din("peer_v", [16384, D])
    c_triU = din("c_triU", [128, 128])
    c_triL = din("c_triL", [128, 128])
    c_ones = din("c_ones", [128, 128])
    c_ident = din("c_ident", [128, 128])
    c_iota = din("c_iota", [128, 16])
    out = nc.dram_tensor("out", [TO, D], F32, kind="ExternalOutput").ap()
    NGR = 20
    uvbf = nc.dram_tensor("uvbf_scr", [16384, 2 * D], BF16).ap()
    wsc = nc.dram_tensor("wsc_scr", [NGR, 128, 16 * 512], BF16).ap()

    es = ExitStack()
    with es:
        P = Prog(nc, es)
        cur = [0]

        def sb(shape, dt, name=None):
            n = int(np.prod(shape[1:])) * DTSIZE[dt]
            n = (n + 31) // 32 * 32
            b = P.sb_at(cur[0], shape, dt, name)
            cur[0] += n
            return b

        wgu = sb([32, 512], BF16, "wgu")
        glr_aug = sb([32, 128], BF16, "glr_aug")
        triU = sb([128, 128], F32, "triU")
        triL = sb([128, 128], F32, "triL")
        ones_f = sb([128, 128], F32, "ones_f")
        ident_f = sb([128, 128], F32, "ident_f")
        iota16 = sb([128, 16], F32, "iota16")
        poolw = sb([128, 4, 2, 256], BF16, "poolw")
        keysT = sb([128, 16, 128], BF16, "keysT")
        wproj = sb([128, 2, D], BF16, "wproj")
        wglr = sb([128, 16, 16], BF16, "wglr")
        lnt = [sb([128, D], F32, "ln%d" % i) for i in range(4)]
        pscale = sb([128, 8], F32, "pscale")
        normw = sb([128, 8], F32, "normw")
        invc = sb([128, 64], F32, "invc")
        S = sb([128, 4, 256], F32, "S")
        S_bf = sb([128, 4, 256], BF16, "S_bf")
        uhalo = sb([128, 8, 16], F32, "uhalo")
        st = sb([128, 4, 6], F32, "st")
        mv = sb([128, 2], F32, "mv")
        sd = sb([128, 2], F32, "sd")
        dec = sb([128, 4], F32, "dec")
        ident_bf = sb([128, 128], BF16, "ident_bf")
        xT_bf = sb([128, 16, 128], BF16, "xT_bf")
        w0_off = cur[0]
        wslot = [sb([128, 16, 512], BF16, "wslot0"), sb([128, 16, 512], BF16, "wslot1")]
        z_off = cur[0]
        z1 = sb([128, D], F32, "z1")
        z2 = sb([128, D], F32, "z2")
        wv1res = Buf(P.nc.alloc_sbuf_tensor_at("wv1res", [128, 16, 512], BF16, offset=P.base + z_off))
        tmpF = sb([128, D], F32, "tmpF")
        tmpG = sb([128, D], F32, "tmpG")
        phase0 = cur[0]
        uT = sb([128, 8, 144], F32, "uT")
        ptA = sb([128, 2, 144], F32, "ptA")
        ptB = sb([128, 2, 144], F32, "ptB")
        pfix = sb([128, 2, 16], F32, "pfix")
        dT = sb([128, 8, 128], BF16, "dT")
        qT = sb([128, 4, 128], F32, "qT")
        kT = sb([128, 4, 128], F32, "kT")
        srT = sb([128, 8, 128], F32, "srT")
        yT = sb([128, 16, 128], BF16, "yT")
        e1 = sb([128, 512], F32, "e1")
        sp_ = sb([128, 512], F32, "sp")
        ek = sb([128, 512], F32, "ek")
        eq = sb([128, 512], F32, "eq")
        ekk = sb([128, 512], F32, "ekk")
        ktl = sb([128, 512], BF16, "ktl")
        v_bf = sb([128, 1024], BF16, "v_bf")
        qtl = sb([128, 4, 128], BF16, "qtl")
        khat = sb([128, 4, 128], BF16, "khat")
        atm = sb([128, 4, 128], BF16, "atm")
        sq = sb([128, 8, 128], F32, "sq")
        on = sb([128, 8, 128], F32, "on")
        rstd = sb([128, 512], F32, "rstd")
        xh_bf = sb([128, 16, 16], BF16, "xh_bf")
        mixer_end = cur[0]
        cur[0] = phase0
        qpT = sb([128, 16, 128], BF16, "qpT")
        S_all_off = cur[0]
        S_all = sb([128, 16, 128], F32, "S_all")
        m16 = sb([128, 16, 16], F32, "m16")
        i16 = sb([128, 16, 16], U32, "i16")
        work = sb([128, 256], F32, "work")
        iF = sb([128, 256], F32, "iF")
        ts = sb([128, 128], F32, "ts")
        sel = sb([128, 128], U32, "sel")
        aU = sb([128, 128], U32, "aU")
        bU = sb([128, 128], U32, "bU")
        aF = sb([128, 128], F32, "aF")
        bF = sb([128, 128], F32, "bF")
        i1g = sb([128, 128], F32, "i1g")
        i2g = sb([128, 128], F32, "i2g")
        eF = sb([128, 128], F32, "eF")
        idx = sb([128, 128], I32, "idx")
        ex = sb([128, 128], F32, "ex")
        gate = sb([128, 128], F32, "gate")
        hbuf = sb([128, 128], F32, "hbuf")
        cbuf = sb([128, 128], F32, "cbuf")
        ssm = sb([128, 8], F32, "ssm")
        sg = sb([128, 512], F32, "sg")
        ptmp = sb([128, 512], F32, "ptmp")
        pT_bf = sb([128, 2, 128], BF16, "pT_bf")
        tmpH_off = cur[0]
        tmpH = sb([128, D], F32, "tmpH")
        gbuf = [sb([128, 2 * D], BF16, "gbuf%d" % i) for i in range(NG)]
        gbuf.append(Buf(P.nc.alloc_sbuf_tensor_at("gbufA", [128, 2 * D], BF16, offset=P.base + S_all_off)))
        gbuf.append(Buf(P.nc.alloc_sbuf_tensor_at("gbufB", [128, 2 * D], BF16, offset=P.base + tmpH_off)))
        diag = [sb([128, 128], BF16, "diag%d" % i) for i in range(4)]
        NHB = 8
        hb = [sb([128, 1], F32, "hb%d" % i) for i in range(NHB)]
        gh = [sb([128, 1], F32, "gh%d" % i) for i in range(NHB)]
        zcol = sb([128, 1], F32, "zcol")
        ffn_end = cur[0]
        assert max(mixer_end, ffn_end) + P.base <= P.top, (mixer_end, ffn_end, P.top - P.base)

        pb = [Buf(es.enter_context(nc.psum_tensor("pb%d" % i, [128, 512], F32))) for i in range(8)]

        def MM(o, lhsT, rhs, start, stop, reads, writes, inc=None):
            P.op("pe", lambda e: e.matmul(o, lhsT, rhs, start=start, stop=stop), reads, writes,
                 inc=stop if inc is None else inc)

        def ACT(o, i, func, reads, writes, **kw):
            P.op("act", lambda e: e.activation(out=o, in_=i, func=func, **kw), reads, writes)

        def TT(eng, o, a, b, op, reads, writes):
            P.op(eng, lambda e: e.tensor_tensor(out=o, in0=a, in1=b, op=op), reads, writes)

        def STT(eng, o, a, s, b, op0, op1, reads, writes, **kw):
            P.op(eng, lambda e: e.scalar_tensor_tensor(out=o, in0=a, scalar=s, in1=b, op0=op0, op1=op1, **kw), reads, writes)

        def TS(eng, o, a, s1, s2, op0, op1, reads, writes):
            if s2 is None:
                P.op(eng, lambda e: e.tensor_scalar(out=o, in0=a, scalar1=s1, scalar2=None, op0=op0), reads, writes)
            else:
                P.op(eng, lambda e: e.tensor_scalar(out=o, in0=a, scalar1=s1, scalar2=s2, op0=op0, op1=op1), reads, writes)

        def CP(eng, o, i, reads, writes):
            P.op(eng, lambda e: e.tensor_copy(out=o, in_=i), reads, writes)

        def wsrc(w, c0, n):
            return w.rearrange("(kc p) n -> p kc n", p=128)[:, :, c0:c0 + n]

        def bcast_row(d):
            return AP(d.tensor, 0, [[0, 128], [1, D]])

        for dst, src in ((triU, c_triU), (triL, c_triL), (ones_f, c_ones), (ident_f, c_ident),
                         (iota16, c_iota), (pscale, pscale_d), (normw, normw_d), (invc, invc_d)):
            P.dma("sp", dst[:], src, writes=[dst], sem_buf=dst)
        for i in range(4):
            P.dma("sp", lnt[i][:], bcast_row(ln_d[i]), writes=[lnt[i]], sem_buf=lnt[i])
        P.dma("pool", wgu[0:17, :], wgu_d, writes=[wgu], sem_buf=wgu)
        P.dma("pool", poolw[:], pool_w.rearrange("g (cc p) d -> p g cc d", p=128), writes=[poolw], sem_buf=poolw)
        P.dma("pool", keysT[:], keysT_d.rearrange("p (a n) -> p a n", a=16), writes=[keysT], sem_buf=keysT)
        P.dma("pool", wproj[:], w_proj.rearrange("(rc p) n -> p rc n", p=128), writes=[wproj], sem_buf=wproj)
        P.dma("pool", wglr[:], wsrc(w_in, C_G, 16), writes=[wglr], sem_buf=wglr)
        P.op("dve", lambda e: e.memset(glr_aug[:], 1.0), writes=[glr_aug])
        CP("dve", ident_bf[:], ident_f[:], [ident_f], [ident_bf])
        P.op("dve", lambda e: e.memset(S[:], 0.0), writes=[S])
        P.op("dve", lambda e: e.memset(S_bf[:], 0.0), writes=[S_bf])
        P.op("dve", lambda e: e.memset(uhalo[:], 0.0), writes=[uhalo])

        def load_xT(src, t0):
            P.dma("pool", xT_bf[:], src.rearrange("(kc p) t -> p kc t", p=128)[:, :, t0:t0 + 128],
                  writes=[xT_bf], sem_buf=xT_bf)

        def step_glr():
            for kc in range(KC):
                MM(pb[6][0:16, 0:128], wglr[:, kc, :], xT_bf[:, kc, :], kc == 0, kc == KC - 1,
                   [wglr, xT_bf], [pb[6]])
            ACT(glr_aug[0:16, :], pb[6][0:16, 0:128], AF.Copy, [pb[6]], [glr_aug])

        def step_ktok(Wk):
            for kc in range(KC):
                MM(pb[2][:, :], xT_bf[:, kc, :], Wk[:, kc, :], kc == 0, kc == KC - 1, [Wk, xT_bf], [pb[2]])

        def step_vtok(Wv, half):
            for kc in range(KC):
                MM(pb[3 + half][:, :], xT_bf[:, kc, :], Wv[:, kc, :], kc == 0, kc == KC - 1,
                   [Wv, xT_bf], [pb[3 + half]])

        def kv_rest(own):
            MM(pb[1][:, :], glr_aug[0:17, :], wgu[0:17, :], True, True, [glr_aug, wgu], [pb[1]])
            ACT(e1[:], pb[1][:, :], AF.Exp, [pb[1]], [e1], scale=-1.0)
            ACT(sp_[:], e1[:], AF.Ln, [e1], [sp_], bias=1.0)
            MM(pb[1][:, :], triL[:], sp_[:], True, True, [triL, sp_], [pb[1]])
            ACT(ek[:], pb[1][:, :], AF.Exp, [pb[1]], [ek], scale=-1.0 / GATE_TEMP)
            TT("dve", ktl[:], pb[2][:, :], ek[:], ALU.mult, [pb[2], ek], [ktl])
            ACT(v_bf[:, 0:512], pb[3][:, :], AF.Copy, [pb[3]], [v_bf])
            ACT(v_bf[:, 512:1024], pb[4][:, :], AF.Copy, [pb[4]], [v_bf])
            for h in range(4):
                MM(pb[5][:, h * 128:(h + 1) * 128], sp_[:, h * 128:(h + 1) * 128], triU[:], True, True,
                   [sp_, triU], [pb[5]])
            ACT(dec[:], pb[5].v(127, (128, 4)), AF.Exp, [pb[5]], [dec], scale=-1.0 / GATE_TEMP)
            if own:
                ACT(eq[:], pb[5][:, :], AF.Exp, [pb[5]], [eq], scale=-1.0 / GATE_TEMP)
                ACT(ekk[:], pb[5][:, :], AF.Exp, [pb[5]], [ekk], scale=1.0 / GATE_TEMP)
                STT("dve", qtl[:], qT[:], float(DK) ** -0.5, eq.v(0, (128, 4), (1, 128)), ALU.mult, ALU.mult,
                    [qT, eq], [qtl])
                TT("dve", khat[:], kT[:], ekk.v(0, (128, 4), (1, 128)), ALU.mult, [kT, ekk], [khat])
                for h in range(4):
                    MM(pb[6][:, h * 128:(h + 1) * 128], khat[:, h, :], qtl[:, h, :], True, True,
                       [khat, qtl], [pb[6]])
                TT("dve", atm[:], pb[6].v(0, (128, 4), (1, 128)), triU.v(0, (0, 4), (1, 128)), ALU.mult,
                   [pb[6], triU], [atm])
                for ch in range(8):
                    h, ec = ch // 2, ch % 2
                    bank = pb[0] if ch < 4 else pb[7]
                    o = bank[:, (ch % 4) * 128:(ch % 4 + 1) * 128]
                    MM(o, v_bf[:, h * 256 + ec * 128:h * 256 + (ec + 1) * 128], atm[:, h, :], True, False,
                       [v_bf, atm], [bank])
                    MM(o, S_bf[:, h, ec * 128:(ec + 1) * 128], qtl[:, h, :], False, True, [S_bf, qtl], [bank])
                ACT(sq[:, 0:4, :], pb[0].v(0, (128, 4), (1, 128)), AF.Square, [pb[0]], [sq])
                ACT(sq[:, 4:8, :], pb[7].v(0, (128, 4), (1, 128)), AF.Square, [pb[7]], [sq])
                for h in range(4):
                    MM(pb[6][:, h * 128:(h + 1) * 128], ones_f[:], sq[:, 2 * h, :], True, False, [ones_f, sq], [pb[6]])
                    MM(pb[6][:, h * 128:(h + 1) * 128], ones_f[:], sq[:, 2 * h + 1, :], False, True, [ones_f, sq], [pb[6]])
                ACT(rstd[:], pb[6][:, :], AF.Sqrt, [pb[6]], [rstd], bias=RMS_EPS, scale=1.0 / 256.0)
                P.op("dve", lambda e: e.reciprocal(out=rstd[:], in_=rstd[:]), [rstd], [rstd])
                TT("dve", on.v(0, (256, 2), (128, 2), (1, 128)), pb[0].v(0, (256, 2), (128, 2), (1, 128)),
                   rstd.v(0, (128, 2), (0, 2), (1, 128)), ALU.mult, [pb[0], rstd], [on])
                TT("dve", on.v(512, (256, 2), (128, 2), (1, 128)), pb[7].v(0, (256, 2), (128, 2), (1, 128)),
                   rstd.v(256, (128, 2), (0, 2), (1, 128)), ALU.mult, [pb[7], rstd], [on])
                for ch in range(8):
                    STT("dve", yT[:, 8 + ch, :], on[:, ch, :], normw[:, ch:ch + 1], srT[:, ch, :], ALU.mult, ALU.mult,
                        [on, normw, srT], [yT])
            for h in range(4):
                bank = pb[3] if h < 2 else pb[4]
                MM(bank[:, (h % 2) * 256:(h % 2 + 1) * 256], ktl[:, h * 128:(h + 1) * 128],
                   v_bf[:, h * 256:(h + 1) * 256], True, True, [ktl, v_bf], [bank])
            for h in range(4):
                bank = pb[3] if h < 2 else pb[4]
                STT("dve", S[:, h, :], S[:, h, :], dec[:, h:h + 1], bank[:, (h % 2) * 256:(h % 2 + 1) * 256],
                    ALU.mult, ALU.add, [S, dec, bank], [S])
            ACT(S_bf[:], S[:], AF.Copy, [S], [S_bf])

        groups = ([(w_in, c) for c in (C_U, C_U + 512, C_Q, C_K, C_V, C_V + 512, C_R, C_R + 512)]
                  + [(w_out, n * 512) for n in range(4)]
                  + [(w_q, n * 512) for n in range(4)]
                  + [(w_g, n * 512) for n in range(4)])
        assert len(groups) == NGR
        total_groups = NGR * NT_OWN
        ws_state = {"issued": 0, "next": 0}
        wsc_b = [DBuf() for _ in range(NGR)]
        uv_b = DBuf()
        cast_jobs = []
        for gi, (w, c0) in enumerate(groups):
            cast_jobs.append((wsc[gi].rearrange("p (kc n) -> p kc n", kc=16), wsrc(w, c0, 512), wsc_b[gi]))
        TCH = 16
        rows = 16384 // TCH
        for i in range(TCH):
            cast_jobs.append((uvbf[i * rows:(i + 1) * rows, 0:D], peer_u[i * rows:(i + 1) * rows, :], uv_b))
            cast_jobs.append((uvbf[i * rows:(i + 1) * rows, D:2 * D], peer_v[i * rows:(i + 1) * rows, :], uv_b))
        cast_state = {"i": 0}

        def cast_issue(n):
            for _ in range(n):
                i = cast_state["i"]
                if i >= len(cast_jobs):
                    return
                dst, src, db = cast_jobs[i]
                P.dma("pool", dst, src, writes=[db], sem_buf=db)
                cast_state["i"] += 1

        if NT_PRE > 0:
            Wk_r, Wv0_r, Wv1_r = wslot[0], wslot[1], wv1res
            P.dma("pool", Wk_r[:], wsrc(w_in, C_K, 512), writes=[Wk_r], sem_buf=Wk_r)
            P.dma("pool", Wv0_r[:], wsrc(w_in, C_V, 512), writes=[Wv0_r], sem_buf=Wv0_r)
            P.dma("pool", Wv1_r[:], wsrc(w_in, C_V + 512, 512), writes=[Wv1_r], sem_buf=Wv1_r)
            cast_issue(NGR)
            for it in range(NT_PRE):
                load_xT(xpT, it * 128)
                cast_issue(1)
                step_glr()
                step_ktok(Wk_r)
                step_vtok(Wv0_r, 0)
                step_vtok(Wv1_r, 1)
                kv_rest(False)
            cast_issue(len(cast_jobs))
            P.barrier()
        else:
            cast_issue(len(cast_jobs))

        def ws_issue():
            i = ws_state["issued"]
            if i >= total_groups:
                return
            gi = i % NGR
            slot = wslot[i % 2]
            P.dma("sp", slot[:], wsc[gi].rearrange("p (kc n) -> p kc n", kc=16), reads=[wsc_b[gi]], writes=[slot],
                  sem_buf=slot)
            ws_state["issued"] += 1

        def ws_get():
            i = ws_state["next"]
            while ws_state["issued"] <= i:
                ws_issue()
            ws_state["next"] += 1
            return wslot[i % 2]

        def ws_prefetch():
            while ws_state["issued"] <= ws_state["next"] and ws_state["issued"] < total_groups:
                ws_issue()

        def LN(Z, lw, lb, O):
            for c in range(4):
                P.op("dve", lambda e, c=c: e.bn_stats(out=st[:, c, :], in_=Z[:, c * 512:(c + 1) * 512]), [Z], [st])
            P.op("dve", lambda e: e.bn_aggr(out=mv[:], in_=st[:]), [st], [mv])
            ACT(sd[:, 0:1], mv[:, 1:2], AF.Sqrt, [mv], [sd], bias=LN_EPS, scale=1.0)
            P.op("dve", lambda e: e.reciprocal(out=sd[:, 1:2], in_=sd[:, 0:1]), [sd], [sd])
            TS("dve", tmpF[:], Z[:], mv[:, 0:1], sd[:, 1:2], ALU.subtract, ALU.mult, [Z, mv, sd], [tmpF])
            TT("dve", tmpF[:], tmpF[:], lw[:], ALU.mult, [tmpF, lw], [tmpF])
            TT("dve", O[:], tmpF[:], lb[:], ALU.add, [tmpF, lb], [O])

        for it in range(NT_OWN):
            t0 = it * 128
            load_xT(xT, t0)
            P.dma("sp", z1[:], xtok[t0:t0 + 128, :], writes=[z1], sem_buf=z1)
            if it == 0:
                P.dma("pool", xh_bf[:], xhT.rearrange("(kc p) t -> p kc t", p=128), writes=[xh_bf], sem_buf=xh_bf)
            else:
                CP("dve", uT[:, :, 0:16], uhalo[:], [uhalo], [uT])

            def wstat(W, nchunk, bank, evac):
                for ci in range(nchunk):
                    for kc in range(KC):
                        MM(bank[:, ci * 128:(ci + 1) * 128], W[:, kc, ci * 128:(ci + 1) * 128], xT_bf[:, kc, :],
                           kc == 0, kc == KC - 1, [W, xT_bf], [bank])
                evac(bank)

            def halo_u(W, cbase):
                for ci in range(4):
                    for kc in range(KC):
                        MM(pb[6][:, ci * 16:(ci + 1) * 16], W[:, kc, ci * 128:(ci + 1) * 128], xh_bf[:, kc, :],
                           kc == 0, kc == KC - 1, [W, xh_bf], [pb[6]])
                ACT(uT[:, cbase:cbase + 4, 0:16], pb[6].v(0, (16, 4), (1, 16)), AF.Copy, [pb[6]], [uT])

            for half in range(2):
                W = ws_get(); ws_prefetch()
                bank = pb[0] if half == 0 else pb[7]
                wstat(W, 4, bank, lambda bk, half=half: ACT(uT[:, half * 4:half * 4 + 4, 16:144],
                                                              bk.v(0, (128, 4), (1, 128)), AF.Copy, [bk], [uT]))
                if it == 0:
                    halo_u(W, half * 4)
            W = ws_get(); ws_prefetch()
            wstat(W, 4, pb[0], lambda bk: ACT(qT[:], bk.v(0, (128, 4), (1, 128)), AF.Copy, [bk], [qT]))
            W = ws_get(); ws_prefetch()
            wstat(W, 4, pb[7], lambda bk: ACT(kT[:], bk.v(0, (128, 4), (1, 128)), AF.Copy, [bk], [kT]))
            step_ktok(W)
            W = ws_get(); ws_prefetch()
            step_vtok(W, 0)
            W = ws_get(); ws_prefetch()
            step_vtok(W, 1)
            step_glr()
            for half in range(2):
                W = ws_get(); ws_prefetch()
                bank = pb[0] if half == 0 else pb[7]
                wstat(W, 4, bank, lambda bk, half=half: ACT(srT[:, half * 4:half * 4 + 4, :],
                                                              bk.v(0, (128, 4), (1, 128)), AF.Silu, [bk], [srT]))
            kv_rest(True)

            CP("dve", uhalo[:], uT[:, :, 128:144], [uT], [uhalo])
            for g in range(4):
                wdw = 2 ** (g + 1)
                src, so = uT, 2 * g * 144
                bufs = [ptA, ptB]
                lo = 0
                for lvl in range(g + 1):
                    sh = 2 ** lvl
                    lo2 = lo + sh
                    dst = bufs[lvl % 2]
                    n = 144 - lo2
                    TT("dve", dst.v(lo2, (144, 2), (1, n)), src.v(so + lo2, (144, 2), (1, n)),
                       src.v(so + lo2 - sh, (144, 2), (1, n)), ALU.add, [src], [dst])
                    src, so, lo = dst, 0, lo2
                STT("dve", dT[:, 2 * g:2 * g + 2, :], src.v(so + 16, (144, 2), (1, 128)), 1.0 / wdw,
                    uT.v(2 * g * 144 + 16, (144, 2), (1, 128)), ALU.mult, ALU.subtract, [src, uT], [dT])
                if it == 0:
                    TT("dve", pfix[:], src.v(so + 16, (144, 2), (1, 16)), invc.v(g * 16, (0, 2), (1, 16)), ALU.mult,
                       [src, invc], [pfix])
                    TT("dve", dT[:, 2 * g:2 * g + 2, 0:16], pfix[:], uT.v(2 * g * 144 + 16, (144, 2), (1, 16)),
                       ALU.subtract, [pfix, uT], [dT])
            for g in range(4):
                for dc in range(2):
                    c = 2 * g + dc
                    bank = pb[5] if c < 4 else pb[6]
                    o = bank[:, (c % 4) * 128:(c % 4 + 1) * 128]
                    for cc in range(2):
                        MM(o, poolw[:, g, cc, dc * 128:(dc + 1) * 128], dT[:, 2 * g + cc, :], cc == 0, cc == 1,
                           [poolw, dT], [bank])
            for c in range(8):
                bank = pb[5] if c < 4 else pb[6]
                TS("dve", yT[:, c, :], bank[:, (c % 4) * 128:(c % 4 + 1) * 128], pscale[:, c:c + 1], None,
                   ALU.mult, None, [bank, pscale], [yT])

            for n in range(4):
                W = ws_get(); ws_prefetch()
                bank = pb[1 + n]
                for c in range(KC):
                    MM(bank[:, :], yT[:, c, :], W[:, c, :], c == 0, c == KC - 1, [yT, W], [bank])
                STT("dve", z1[:, n * 512:(n + 1) * 512], z1[:, n * 512:(n + 1) * 512], ALPHA, bank[:, :],
                    ALU.mult, ALU.add, [z1, bank], [z1])
            P.barrier()

            P.dma("pool", pT_bf[:], pT.rearrange("(rc p) t -> p rc t", p=128)[:, :, t0:t0 + 128],
                  writes=[pT_bf], sem_buf=pT_bf)
            LN(z1, lnt[0], lnt[1], z1)
            for c in range(KC):
                bank = pb[1 + c // 4]
                P.op("pe", lambda e, c=c, bank=bank: e.transpose(out=bank[:, (c % 4) * 128:(c % 4 + 1) * 128],
                                                                  in_=z1[:, c * 128:(c + 1) * 128], identity=ident_f[:]),
                     [z1, ident_f], [bank])
            for q4 in range(4):
                ACT(xT_bf[:, q4 * 4:q4 * 4 + 4, :], pb[1 + q4].v(0, (128, 4), (1, 128)), AF.Copy, [pb[1 + q4]], [xT_bf])

            for n in range(4):
                W = ws_get(); ws_prefetch()
                bank = pb[5 + n % 2]
                for ci in range(4):
                    for kc in range(KC):
                        MM(bank[:, ci * 128:(ci + 1) * 128], W[:, kc, ci * 128:(ci + 1) * 128], xT_bf[:, kc, :],
                           kc == 0, kc == KC - 1, [W, xT_bf], [bank])
                ACT(qpT[:, n * 4:n * 4 + 4, :], bank.v(0, (128, 4), (1, 128)), AF.Copy, [bank], [qpT])
            for hp in range(16):
                bank = pb[1 + hp // 4]
                MM(bank[:, (hp % 4) * 128:(hp % 4 + 1) * 128], qpT[:, hp, :], keysT[:, hp, :], True, True,
                   [qpT, keysT], [bank])
            for q4 in range(4):
                ACT(S_all[:, q4 * 4:q4 * 4 + 4, :], pb[1 + q4].v(0, (128, 4), (1, 128)), AF.Copy, [pb[1 + q4]], [S_all])

            def ple_iter(n):
                W = ws_get(); ws_prefetch()
                bg = pb[1 + n % 2]
                bp = pb[3 + n % 2]
                for kc in range(KC):
                    MM(bg[:, :], xT_bf[:, kc, :], W[:, kc, :], kc == 0, kc == KC - 1, [xT_bf, W], [bg])
                for rc in range(2):
                    MM(bp[:, :], pT_bf[:, rc, :], wproj[:, rc, n * 512:(n + 1) * 512], rc == 0, rc == 1,
                       [pT_bf, wproj], [bp])
                ACT(sg[:], bg[:, :], AF.Sigmoid, [bg], [sg])
                TT("dve", ptmp[:], sg[:], bp[:, :], ALU.mult, [sg, bp], [ptmp])
                STT("dve", z2[:, n * 512:(n + 1) * 512], z1[:, n * 512:(n + 1) * 512], ALPHA, ptmp[:],
                    ALU.mult, ALU.add, [z1, ptmp], [z2])

            ple_iter(0)
            m16s = [Buf(m16.t) for _ in range(16)]
            i16s = [Buf(i16.t) for _ in range(16)]
            w16s = [Buf(tmpG.t) for _ in range(16)]

            def W16(hp):
                return tmpG[:, hp * 128:(hp + 1) * 128]
            for hp in range(16):
                P.op("dve", lambda e, hp=hp: e.max(out=m16[:, hp, 0:8], in_=S_all[:, hp, :]), [S_all], [m16s[hp]])
            for hp in range(16):
                P.op("dve", lambda e, hp=hp: e.max_index(out=i16[:, hp, 0:8], in_max=m16[:, hp, 0:8],
                                                          in_values=S_all[:, hp, :]), [S_all, m16s[hp]], [i16s[hp]])
            for hp in range(16):
                P.op("dve", lambda e, hp=hp: e.match_replace(out=W16(hp), in_to_replace=m16[:, hp, 0:8],
                                                              in_values=S_all[:, hp, :], imm_value=NEG),
                     [S_all, m16s[hp]], [w16s[hp]])
            for hp in range(16):
                P.op("dve", lambda e, hp=hp: e.max(out=m16[:, hp, 8:16], in_=W16(hp)), [w16s[hp]], [m16s[hp]])
            for hp in range(16):
                P.op("dve", lambda e, hp=hp: e.max_index(out=i16[:, hp, 8:16], in_max=m16[:, hp, 8:16],
                                                          in_values=W16(hp)), [w16s[hp], m16s[hp]], [i16s[hp]])
            ple_iter(1)
            TT("dve", tmpF.v(0, (256, 8), (16, 16), (1, 16)), m16.v(0, (32, 8), (1, 16), (0, 16)),
               m16.v(16, (32, 8), (0, 16), (1, 16)), ALU.add, m16s, [tmpF])
            ts_s = [Buf(ts.t) for _ in range(8)]
            sel_s = [Buf(sel.t) for _ in range(8)]
            w8s = [Buf(tmpH.t) for _ in range(8)]

            def C8(h):
                return tmpF[:, h * 256:(h + 1) * 256]

            def W8(h):
                return tmpH[:, h * 256:(h + 1) * 256]
            for h in range(8):
                P.op("dve", lambda e, h=h: e.max(out=ts[:, h * 16:h * 16 + 8], in_=C8(h)), [tmpF], [ts_s[h]])
            for h in range(8):
                P.op("dve", lambda e, h=h: e.max_index(out=sel[:, h * 16:h * 16 + 8], in_max=ts[:, h * 16:h * 16 + 8],
                                                        in_values=C8(h)), [tmpF, ts_s[h]], [sel_s[h]])
            for h in range(8):
                P.op("dve", lambda e, h=h: e.match_replace(out=W8(h), in_to_replace=ts[:, h * 16:h * 16 + 8],
                                                            in_values=C8(h), imm_value=NEG), [tmpF, ts_s[h]], [w8s[h]])
            for h in range(8):
                P.op("dve", lambda e, h=h: e.max(out=ts[:, h * 16 + 8:h * 16 + 16], in_=W8(h)), [w8s[h]], [ts_s[h]])
            for h in range(8):
                P.op("dve", lambda e, h=h: e.max_index(out=sel[:, h * 16 + 8:h * 16 + 16],
                                                        in_max=ts[:, h * 16 + 8:h * 16 + 16], in_values=W8(h)),
                     [w8s[h], ts_s[h]], [sel_s[h]])
            P.op("dve", lambda e: e.tensor_copy(out=ssm[:, 0:1], in_=ts[:, 0:1]),
                 ts_s + sel_s + w8s + w16s + i16s + m16s, [ssm, ts, sel, tmpH, tmpG, i16, m16])
            ple_iter(2)
            P.op("dve", lambda e: e.tensor_single_scalar(out=bU[:], in_=sel[:], scalar=15, op=ALU.bitwise_and), [sel], [bU])
            P.op("dve", lambda e: e.tensor_single_scalar(out=aU[:], in_=sel[:], scalar=4, op=ALU.logical_shift_right), [sel], [aU])
            CP("dve", aF[:], aU[:], [aU], [aF])
            CP("dve", bF[:], bU[:], [bU], [bF])
            CP("dve", iF[:], i16.v(0, (1, 256)), [i16], [iF])
            for which, xF, off, dst in ((0, aF, 0, i1g), (1, bF, 16, i2g)):
                TT("dve", tmpG.v(0, (256, 8), (16, 16), (1, 16)), xF.v(0, (16, 8), (1, 16), (0, 16)),
                   iota16.v(0, (0, 8), (0, 16), (1, 16)), ALU.is_equal, [xF, iota16], [tmpG])
                TT("dve", tmpH.v(0, (256, 8), (16, 16), (1, 16)), tmpG.v(0, (256, 8), (16, 16), (1, 16)),
                   iF.v(off, (32, 8), (0, 16), (1, 16)), ALU.mult, [tmpG, iF], [tmpH])
                P.op("dve", lambda e, dst=dst: e.reduce_sum(out=dst.v(0, (16, 8), (1, 16)),
                                                             in_=tmpH.v(0, (256, 8), (16, 16), (1, 16)), axis=AX.X),
                     [tmpH], [dst])
            STT("dve", eF[:], i1g[:], 128.0, i2g[:], ALU.mult, ALU.add, [i1g, i2g], [eF])
            CP("dve", idx[:], eF[:], [eF], [idx])
            ple_iter(3)
            TT("dve", ex.v(0, (16, 8), (1, 16)), ts.v(0, (16, 8), (1, 16)), ts.v(0, (16, 8), (0, 16)), ALU.subtract, [ts], [ex])
            ACT(ex[:], ex[:], AF.Exp, [ex], [ex])
            P.op("dve", lambda e: e.reduce_sum(out=ssm[:], in_=ex.v(0, (16, 8), (1, 16)), axis=AX.X), [ex], [ssm])
            P.op("dve", lambda e: e.reciprocal(out=ssm[:], in_=ssm[:]), [ssm], [ssm])
            TT("dve", gate.v(0, (16, 8), (1, 16)), ex.v(0, (16, 8), (1, 16)), ssm.v(0, (1, 8), (0, 16)), ALU.mult, [ex, ssm], [gate])

            NGB = len(gbuf)
            P.op("dve", lambda e: e.memset(zcol[:], 0.0), [], [zcol])
            for i4 in range(NHB):
                P.op("dve", lambda e, i4=i4: e.memset(hb[i4][:], 0.0), [], [hb[i4]])

            def issue_gather(k):
                g = gbuf[k % NGB]

                def fn(e, g=g, k=k):
                    return e.indirect_dma_start(out=g[:, :], out_offset=None, in_=uvbf[:, :],
                                                in_offset=bass.IndirectOffsetOnAxis(ap=idx[:, k:k + 1], axis=0))
                P.dma("pool", None, None, reads=[idx, uv_b], writes=[g], sem_buf=g, fn=fn)

            def dot(k):
                g = gbuf[k % NGB]
                h_, g_ = hb[k % NHB], gh[k % NHB]
                jk = (tmpF, tmpG)[k % 2]
                STT("dve", jk[:], g[:, 0:D], 1.0, z1[:], ALU.mult, ALU.mult, [g, z1], [jk, h_],
                    accum_out=h_[:])
                ACT(g_[:], h_[:], AF.Gelu, [h_], [g_])
                ACT(h_[:], zcol[:], AF.Copy, [zcol, g_], [h_])
                ACT(g_[:], g_[:], AF.Copy, [g_, gate], [g_], scale=gate[:, k:k + 1])
                ACT(diag[k % 4][:], ident_bf[:], AF.Copy, [ident_bf, g_], [diag[k % 4]], scale=g_[:, 0:1])

            def accum(k):
                g = gbuf[k % NGB]
                dg = diag[k % 4]
                for n in range(4):
                    MM(pb[1 + n][:, :], dg[:], g[:, D + n * 512:D + (n + 1) * 512], k == 0, k == 127,
                       [dg, g], [pb[1 + n]], inc=(n == 3 or k == 127))

            for k in range(NGB - 1):
                issue_gather(k)
            for k in range(128):
                dot(k)
                if k >= 1:
                    accum(k - 1)
                if k + NGB - 1 < 128:
                    issue_gather(k + NGB - 1)
            accum(127)
            for n in range(4):
                TT("dve", z2[:, n * 512:(n + 1) * 512], z2[:, n * 512:(n + 1) * 512], pb[1 + n][:, :], ALU.add,
                   [z2, pb[1 + n]], [z2])

            LN(z2, lnt[2], lnt[3], z2)
            P.dma("sp", out[t0:t0 + 128, :], z2[:], reads=[z2], sem_buf=z2, final=True)
            P.barrier()
        P.emit()
    return nc


def make_consts():
    s = np.arange(128)
    triU = (s[:, None] <= s[None, :]).astype(np.float32)
    triL = (s[:, None] > s[None, :]).astype(np.float32)
    return {"c_triU": triU, "c_triL": triL, "c_ones": np.ones((128, 128), np.float32),
            "c_ident": np.eye(128, dtype=np.float32),
            "c_iota": np.tile(np.arange(16, dtype=np.float32)[None, :], (128, 1))}


def shared_inputs(w_in, gla_w_gate_up, gla_b_gate, gla_norm_w, pool_w, pool_scale, w_out, ln1_w, ln1_b,
                  peer_w_query, peer_sub_keys, peer_u, peer_v, ple_w_gate, ple_w_proj, ln2_w, ln2_b):
    f = lambda a: np.ascontiguousarray(np.asarray(a, dtype=np.float32))
    m = dict(make_consts())
    m["w_in"] = f(w_in[0]); m["w_out"] = f(w_out[0]); m["w_q"] = f(peer_w_query[0]); m["w_g"] = f(ple_w_gate[0])
    m["w_proj"] = f(ple_w_proj[0]); m["pool_w"] = f(pool_w[0])
    m["wgu"] = f(np.concatenate([np.asarray(gla_w_gate_up[0]), np.asarray(gla_b_gate[0])[None, :]], axis=0))
    m["keysT"] = f(np.asarray(peer_sub_keys[0]).transpose(3, 0, 1, 2).reshape(128, 16 * 128))
    m["pscale"] = f(np.asarray(pool_scale[0]).reshape(8, 128).T)
    m["normw"] = f(np.asarray(gla_norm_w[0]).reshape(8, 128).T)
    m["ln1w"] = f(np.asarray(ln1_w[0])[None, :]); m["ln1b"] = f(np.asarray(ln1_b[0])[None, :])
    m["ln2w"] = f(np.asarray(ln2_w[0])[None, :]); m["ln2b"] = f(np.asarray(ln2_b[0])[None, :])
    m["peer_u"] = f(peer_u[0]); m["peer_v"] = f(peer_v[0])
    return m


def core_inputs(xb, pb_, s0, TO, TP):
    f = lambda a: np.ascontiguousarray(a, dtype=np.float32)
    m = {}
    m["xtok"] = f(xb[s0:s0 + TO])
    m["xT"] = f(xb[s0:s0 + TO].T)
    TPA = max(TP, 128)
    xp = np.zeros((D, TPA), np.float32)
    npre = min(s0, TP)
    if npre > 0:
        xp[:, TPA - npre:] = xb[s0 - npre:s0].T
    m["xpT"] = xp
    xh = np.zeros((D, 16), np.float32)
    if s0 >= 16:
        xh[:, :] = xb[s0 - 16:s0].T
    m["xhT"] = xh
    m["pT"] = f(pb_[s0:s0 + TO].T)
    ic = np.zeros((128, 64), np.float32)
    for g in range(4):
        w = 2 ** (g + 1)
        for t in range(16):
            ic[:, g * 16 + t] = 1.0 / (min(t + 1, w) if s0 == 0 else w)
    m["invc"] = ic
    return m


_NC_CACHE = {}


def kernel(x, p, w_in, gla_w_gate_up, gla_b_gate, gla_norm_w, pool_w, pool_scale, w_out, ln1_w, ln1_b,
           peer_w_query, peer_sub_keys, peer_u, peer_v, ple_w_gate, ple_w_proj, ln2_w, ln2_b):
    x = np.asarray(x, dtype=np.float32)
    p = np.asarray(p, dtype=np.float32)
    B, S, _ = x.shape
    NCORE = 8
    per = (B * S) // NCORE
    cps = S // per
    NT_OWN = per // 128
    NT_PRE = (cps - 1) * NT_OWN
    sh = shared_inputs(w_in, gla_w_gate_up, gla_b_gate, gla_norm_w, pool_w, pool_scale, w_out, ln1_w, ln1_b,
                       peer_w_query, peer_sub_keys, peer_u, peer_v, ple_w_gate, ple_w_proj, ln2_w, ln2_b)
    in_maps = []
    for c in range(NCORE):
        b, j = c // cps, c % cps
        m = dict(sh)
        m.update(core_inputs(x[b], p[0, b], j * per, per, NT_PRE * 128))
        in_maps.append(m)
    key = (NT_OWN, NT_PRE)
    if key not in _NC_CACHE:
        _NC_CACHE[key] = build_program(NT_OWN, NT_PRE)
    nc = _NC_CACHE[key]
    res = run_bass_kernel_spmd(nc, in_maps, core_ids=list(range(NCORE)))
    outs = [np.asarray(r["out"], dtype=np.float32) for r in res.results]
    return np.concatenate(outs, axis=0).reshape(B, S, D)
```
